# Optimizing a Trainium2 kernel written in Bass

```python
import jax, jax.numpy as jnp
from jax import lax
import numpy as np

D_MODEL = 2048
BATCH = 1
SEQ = 8192
DEPTH = 2

EPS = 1e-5
N_MEM = 256
D_FF = 5632
GM_CHUNK = 128
GM_GROUPS = 4
GM_WIDTH = D_MODEL
GM_GDIM = GM_WIDTH // GM_GROUPS
SSD_WIDTH = D_MODEL
SSD_HEAD_DIM = 64
SSD_HEADS = SSD_WIDTH // SSD_HEAD_DIM
SSD_GROUPS = 4
SSD_HPG = SSD_HEADS // SSD_GROUPS
SSD_STATE = 128
SSD_CONV = 4
SSD_CHUNK = 128
SSD_BC = SSD_GROUPS * SSD_STATE
SSD_CONV_DIM = SSD_WIDTH + 2 * SSD_BC
EVEN_IN = 2 * GM_WIDTH + SSD_WIDTH + SSD_CONV_DIM + SSD_HEADS
EVEN_MIX = GM_WIDTH + SSD_WIDTH
ATT_HEADS = 32
ATT_KV_HEADS = 4
ATT_HEAD_DIM = D_MODEL // ATT_HEADS
ATT_REP = ATT_HEADS // ATT_KV_HEADS
WINDOW = 128
ATT_SCALE = ATT_HEAD_DIM ** -0.5
ROT_DIM = ATT_HEAD_DIM // 4
ROPE_THETA = 500000.0
ODD_IN = (ATT_HEADS + 2 * ATT_KV_HEADS) * ATT_HEAD_DIM
X_HEADS = 4
X_HEAD_DIM = 128
X_WIDTH = X_HEADS * X_HEAD_DIM
X_SCALE = X_HEAD_DIM ** -0.5
N_EVEN = (DEPTH + 1) // 2
N_ODD = DEPTH // 2

kernel_name = 'hybrid_gmlp_ssd_swa_macaron'


def rmsnorm(x, g):
    xf = x.astype(jnp.float32)
    y = xf * lax.rsqrt(jnp.mean(xf * xf, -1, keepdims=True) + EPS)
    return (y * g.astype(jnp.float32)).astype(x.dtype)


def swiglu(h, w_gu, w_down):
    g, u = jnp.split(h @ w_gu, 2, axis=-1)
    return (jax.nn.silu(g) * u) @ w_down


def chunked_gmlp(uv, ln_g, ln_b, w_s, b_s):
    bsz, L, _ = uv.shape
    nc = L // GM_CHUNK
    u, v = jnp.split(jax.nn.gelu(uv, approximate=False), 2, axis=-1)
    vf = v.reshape(bsz, nc, GM_CHUNK, GM_GROUPS, GM_GDIM).astype(jnp.float32)
    mu = jnp.mean(vf, -1, keepdims=True)
    var = jnp.mean(jnp.square(vf - mu), -1, keepdims=True)
    vn = ((vf - mu) * lax.rsqrt(var + EPS)).astype(v.dtype)
    vn = vn * ln_g.reshape(GM_GROUPS, GM_GDIM) + ln_b.reshape(GM_GROUPS, GM_GDIM)
    causal = jnp.tril(jnp.ones((GM_CHUNK, GM_CHUNK), dtype=bool))
    ws = jnp.where(causal[None], w_s, 0.0)
    s = jnp.einsum('gij,bcjgd->bcigd', ws, vn) + b_s.T[None, None, :, :, None]
    return u * s.reshape(bsz, L, GM_WIDTH)


def causal_dwconv(x, w, b):
    K = w.shape[0]
    L = x.shape[1]
    xp = jnp.pad(x, ((0, 0), (K - 1, 0), (0, 0)))
    y = xp[:, 0:L] * w[0]
    for k in range(1, K):
        y = y + xp[:, k:k + L] * w[k]
    return y + b


def ssd_scan(xs, dt, A, Bm, Cm):
    bsz, L = xs.shape[:2]
    nc = L // SSD_CHUNK
    Q, G, R, P, N = SSD_CHUNK, SSD_GROUPS, SSD_HPG, SSD_HEAD_DIM, SSD_STATE
    x = xs.astype(jnp.float32).reshape(bsz, nc, Q, G, R, P)
    dtc = dt.reshape(bsz, nc, Q, G, R)
    Bc = Bm.astype(jnp.float32).reshape(bsz, nc, Q, G, N)
    Cc = Cm.astype(jnp.float32).reshape(bsz, nc, Q, G, N)
    a = jnp.moveaxis(dtc * A.reshape(G, R), 2, -1)
    a_cs = jnp.cumsum(a, axis=-1)
    xdt = x * dtc[..., None]
    causal = jnp.tril(jnp.ones((Q, Q), dtype=bool))
    seg = a_cs[..., :, None] - a_cs[..., None, :]
    Lmat = jnp.where(causal, jnp.exp(jnp.where(causal, seg, 0.0)), 0.0)
    cb = jnp.einsum('bcign,bcjgn->bcgij', Cc, Bc)
    y_diag = jnp.einsum('bcgij,bcgrij,bcjgrp->bcigrp', cb, Lmat, xdt)
    decay_states = jnp.exp(a_cs[..., -1:] - a_cs)
    states = jnp.einsum('bcjgn,bcgrj,bcjgrp->bcgrpn', Bc, decay_states, xdt)
    chunk_decay = jnp.exp(a_cs[..., -1])

    def step(h, inp):
        s_c, d_c = inp
        return h * d_c[..., None, None] + s_c, h

    h0 = jnp.zeros((bsz, G, R, P, N), jnp.float32)
    _, prev = lax.scan(step, h0, (jnp.moveaxis(states, 1, 0), jnp.moveaxis(chunk_decay, 1, 0)))
    prev = jnp.moveaxis(prev, 0, 1)
    y_off = jnp.einsum('bcign,bcgrpn,bcgri->bcigrp', Cc, prev, jnp.exp(a_cs))
    return (y_diag + y_off).reshape(bsz, L, G * R * P)


def even_mixer(h, w_in, gm_ln_g, gm_ln_b, gm_ws, gm_bs, conv_w, conv_b, dt_bias, a_log, d_skip, ssd_norm, w_out):
    bsz, L, _ = h.shape
    proj = h @ w_in
    c0 = 2 * GM_WIDTH
    c1 = c0 + SSD_WIDTH
    c2 = c1 + SSD_CONV_DIM
    uv, z, xbc, dt_raw = jnp.split(proj, [c0, c1, c2], axis=-1)
    a_out = chunked_gmlp(uv, gm_ln_g, gm_ln_b, gm_ws, gm_bs)
    xbc = jax.nn.silu(causal_dwconv(xbc, conv_w, conv_b))
    xs, Bm, Cm = jnp.split(xbc, [SSD_WIDTH, SSD_WIDTH + SSD_BC], axis=-1)
    dt = jax.nn.softplus(dt_raw.astype(jnp.float32) + dt_bias.astype(jnp.float32))
    A = -jnp.exp(a_log.astype(jnp.float32))
    xh = xs.reshape(bsz, L, SSD_HEADS, SSD_HEAD_DIM)
    y = ssd_scan(xh, dt, A, Bm.reshape(bsz, L, SSD_GROUPS, SSD_STATE), Cm.reshape(bsz, L, SSD_GROUPS, SSD_STATE))
    y = y + (xh.astype(jnp.float32) * d_skip.astype(jnp.float32)[:, None]).reshape(bsz, L, SSD_WIDTH)
    yg = (y * jax.nn.silu(z.astype(jnp.float32))).reshape(bsz, L, SSD_GROUPS, SSD_WIDTH // SSD_GROUPS)
    yg = yg * lax.rsqrt(jnp.mean(yg * yg, -1, keepdims=True) + EPS)
    b_out = (yg.reshape(bsz, L, SSD_WIDTH) * ssd_norm.astype(jnp.float32)).astype(h.dtype)
    return jnp.concatenate([a_out, b_out], axis=-1) @ w_out


def rope_partial(x, cos, sin):
    half = ROT_DIM // 2
    x1 = x[..., :half]
    x2 = x[..., half:ROT_DIM]
    return jnp.concatenate([x1 * cos - x2 * sin, x2 * cos + x1 * sin, x[..., ROT_DIM:]], axis=-1)


def swa_sinks(h, w_qkv, b_qkv, sinks, w_o, cos, sin):
    bsz, L, _ = h.shape
    nb = L // WINDOW
    W = WINDOW
    qkv = h @ w_qkv + b_qkv
    q, k, v = jnp.split(qkv, [ATT_HEADS * ATT_HEAD_DIM, (ATT_HEADS + ATT_KV_HEADS) * ATT_HEAD_DIM], axis=-1)
    q = rope_partial(q.reshape(bsz, L, ATT_HEADS, ATT_HEAD_DIM), cos, sin)
    k = rope_partial(k.reshape(bsz, L, ATT_KV_HEADS, ATT_HEAD_DIM), cos, sin)
    q = q.reshape(bsz, nb, W, ATT_KV_HEADS, ATT_REP, ATT_HEAD_DIM)
    kb = k.reshape(bsz, nb, W, ATT_KV_HEADS, ATT_HEAD_DIM)
    vb = v.reshape(bsz, nb, W, ATT_KV_HEADS, ATT_HEAD_DIM)
    pad = ((0, 0), (1, 0), (0, 0), (0, 0), (0, 0))
    kcat = jnp.concatenate([jnp.pad(kb, pad)[:, :-1], kb], axis=2)
    vcat = jnp.concatenate([jnp.pad(vb, pad)[:, :-1], vb], axis=2)
    s = jnp.einsum('bnqkrd,bnskd->bnkrqs', q, kcat).astype(jnp.float32) * ATT_SCALE
    iq = jnp.arange(W)[:, None]
    js = jnp.arange(2 * W)[None, :]
    rel = iq + W - js
    band = (rel >= 0) & (rel < WINDOW)
    blk = jnp.arange(nb)[:, None, None]
    mask = band[None] & ((blk > 0) | (js >= W)[None])
    s = jnp.where(mask[None, :, None, None], s, -jnp.inf)
    sink = sinks.astype(jnp.float32).reshape(ATT_KV_HEADS, ATT_REP)[None, None, :, :, None, None]
    m = jnp.maximum(jnp.max(s, -1, keepdims=True), sink)
    p = jnp.exp(s - m)
    pr = (p / (jnp.sum(p, -1, keepdims=True) + jnp.exp(sink - m))).astype(vcat.dtype)
    o = jnp.einsum('bnkrqs,bnskd->bnqkrd', pr, vcat).reshape(bsz, L, ATT_HEADS * ATT_HEAD_DIM)
    return o @ w_o


def mem_cross_attn(h, mem_n, w_q, w_kv, w_o):
    bsz, L, _ = h.shape
    q = (h @ w_q).reshape(bsz, L, X_HEADS, X_HEAD_DIM)
    k, v = jnp.split(mem_n @ w_kv, 2, axis=-1)
    k = k.reshape(bsz, -1, X_HEADS, X_HEAD_DIM)
    v = v.reshape(bsz, -1, X_HEADS, X_HEAD_DIM)
    s = jnp.einsum('blhd,bmhd->bhlm', q, k).astype(jnp.float32) * X_SCALE
    p = jax.nn.softmax(s, axis=-1).astype(v.dtype)
    o = jnp.einsum('bhlm,bmhd->blhd', p, v).reshape(bsz, L, X_WIDTH)
    return o @ w_o


def setup_inputs(seed: int = 0) -> dict:
    key = jax.random.key(seed)
    ks = list(jax.random.split(key, 40))
    f32 = jnp.float32

    def nrm(i, shape, scale):
        return jax.random.normal(ks[i], shape, f32) * scale

    def gain(i, shape):
        return 1.0 + 0.02 * jax.random.normal(ks[i], shape, f32)

    x = nrm(0, (BATCH, SEQ, D_MODEL), 1.0)
    mem = nrm(1, (BATCH, N_MEM, D_MODEL), 1.0)
    start = jax.random.randint(ks[2], (BATCH, 1), 0, 4096, dtype=jnp.int32)
    positions = start + jnp.arange(SEQ, dtype=jnp.int32)[None, :]
    dt0 = jnp.exp(jax.random.uniform(ks[20], (N_EVEN, SSD_HEADS), f32, np.log(1e-3), np.log(1e-1)))
    return {
        'x': x,
        'mem': mem,
        'positions': positions,
        'norm_ffn1': gain(3, (DEPTH, D_MODEL)),
        'w_ffn1_gu': nrm(4, (DEPTH, D_MODEL, 2 * D_FF), D_MODEL ** -0.5),
        'w_ffn1_down': nrm(5, (DEPTH, D_FF, D_MODEL), D_FF ** -0.5),
        'norm_mix': gain(6, (DEPTH, D_MODEL)),
        'w_in_even': nrm(7, (N_EVEN, D_MODEL, EVEN_IN), D_MODEL ** -0.5),
        'gm_ln_g': gain(8, (N_EVEN, GM_WIDTH)),
        'gm_ln_b': nrm(9, (N_EVEN, GM_WIDTH), 0.02),
        'gm_ws': nrm(10, (N_EVEN, GM_GROUPS, GM_CHUNK, GM_CHUNK), GM_CHUNK ** -0.5),
        'gm_bs': 1.0 + nrm(11, (N_EVEN, GM_GROUPS, GM_CHUNK), 0.1),
        'conv_w': nrm(12, (N_EVEN, SSD_CONV, SSD_CONV_DIM), SSD_CONV ** -0.5),
        'conv_b': nrm(13, (N_EVEN, SSD_CONV_DIM), 0.02),
        'dt_bias': dt0 + jnp.log(-jnp.expm1(-dt0)),
        'a_log': jnp.log(jax.random.uniform(ks[14], (N_EVEN, SSD_HEADS), f32, 1.0, 16.0)),
        'd_skip': 1.0 + nrm(15, (N_EVEN, SSD_HEADS), 0.1),
        'ssd_norm': gain(16, (N_EVEN, SSD_WIDTH)),
        'w_out_even': nrm(17, (N_EVEN, EVEN_MIX, D_MODEL), EVEN_MIX ** -0.5),
        'w_qkv': nrm(18, (N_ODD, D_MODEL, ODD_IN), D_MODEL ** -0.5),
        'b_qkv': nrm(19, (N_ODD, ODD_IN), 0.02),
        'sinks': nrm(21, (N_ODD, ATT_HEADS), 0.5),
        'w_o_odd': nrm(22, (N_ODD, ATT_HEADS * ATT_HEAD_DIM, D_MODEL), (ATT_HEADS * ATT_HEAD_DIM) ** -0.5),
        'norm_xq': gain(23, (DEPTH, D_MODEL)),
        'norm_mem': gain(24, (DEPTH, D_MODEL)),
        'w_xq': nrm(25, (DEPTH, D_MODEL, X_WIDTH), D_MODEL ** -0.5),
        'w_xkv': nrm(26, (DEPTH, D_MODEL, 2 * X_WIDTH), D_MODEL ** -0.5),
        'w_xo': nrm(27, (DEPTH, X_WIDTH, D_MODEL), X_WIDTH ** -0.5),
        'norm_ffn2': gain(28, (DEPTH, D_MODEL)),
        'w_ffn2_gu': nrm(29, (DEPTH, D_MODEL, 2 * D_FF), D_MODEL ** -0.5),
        'w_ffn2_down': nrm(30, (DEPTH, D_FF, D_MODEL), D_FF ** -0.5),
        'final_norm': gain(31, (D_MODEL,)),
    }


def reference(x, mem, positions, norm_ffn1, w_ffn1_gu, w_ffn1_down, norm_mix, w_in_even, gm_ln_g, gm_ln_b,
              gm_ws, gm_bs, conv_w, conv_b, dt_bias, a_log, d_skip, ssd_norm, w_out_even, w_qkv, b_qkv, sinks,
              w_o_odd, norm_xq, norm_mem, w_xq, w_xkv, w_xo, norm_ffn2, w_ffn2_gu, w_ffn2_down, final_norm):
    inv_freq = ROPE_THETA ** (-jnp.arange(0, ROT_DIM, 2, dtype=jnp.float32) / ROT_DIM)
    ang = positions.astype(jnp.float32)[..., None] * inv_freq
    cos = jnp.cos(ang)[:, :, None, :].astype(x.dtype)
    sin = jnp.sin(ang)[:, :, None, :].astype(x.dtype)
    for i in range(DEPTH):
        j = i // 2
        x = x + 0.5 * swiglu(rmsnorm(x, norm_ffn1[i]), w_ffn1_gu[i], w_ffn1_down[i])
        h = rmsnorm(x, norm_mix[i])
        if i % 2 == 0:
            x = x + even_mixer(h, w_in_even[j], gm_ln_g[j], gm_ln_b[j], gm_ws[j], gm_bs[j], conv_w[j], conv_b[j],
                               dt_bias[j], a_log[j], d_skip[j], ssd_norm[j], w_out_even[j])
        else:
            x = x + swa_sinks(h, w_qkv[j], b_qkv[j], sinks[j], w_o_odd[j], cos, sin)
        x = x + mem_cross_attn(rmsnorm(x, norm_xq[i]), rmsnorm(mem, norm_mem[i]), w_xq[i], w_xkv[i], w_xo[i])
        x = x + 0.5 * swiglu(rmsnorm(x, norm_ffn2[i]), w_ffn2_gu[i], w_ffn2_down[i])
    return rmsnorm(x, final_norm)
```

```python
import contextlib
import numpy as np
import ml_dtypes
import concourse.bass as bass
import concourse.mybir as mybir
from concourse.bass_utils import run_bass_kernel_spmd

BF16_NP = ml_dtypes.bfloat16
F32 = mybir.dt.float32
BF16 = mybir.dt.bfloat16
I32 = mybir.dt.int32
AF = mybir.ActivationFunctionType
ALU = mybir.AluOpType

NCORES = 8
D = 2048
KC = D // 128
SEQ = 8192
T = SEQ // NCORES
TH = 512
NTH = T // TH
DFF = 5632
NFF = DFF // 128
FF_GROUPS = 4
FPG = NFF // FF_GROUPS
EPS = 1e-5
WSLOT = 2048
NWSLOT = 6


class Prog:
    ENGS = ("pe", "act", "dve", "pool", "sp")

    def __init__(self, nc, dry=False):
        self.nc = nc
        self.dry = dry
        self.ops = {e: [] for e in self.ENGS}
        self.last_w = {}
        self.readers = {}
        self.dma_cnt = {}
        self.nsig = {e: 0 for e in self.ENGS}

    def _deps(self, reads, writes):
        deps = []
        for r in reads:
            w = self.last_w.get(r)
            if w is not None:
                deps.append(w)
        for w_ in writes:
            lw = self.last_w.get(w_)
            if lw is not None:
                deps.append(lw)
            for rd in self.readers.get(w_, {}).values():
                deps.append(rd)
        return deps

    def _mark(self, reads, writes, tok, rkey):
        for w_ in writes:
            self.last_w[w_] = tok
            self.readers[w_] = {}
        for r in reads:
            self.readers.setdefault(r, {})[rkey] = tok

    def add(self, eng, fn, reads=(), writes=(), sig=None):
        if self.dry:
            return
        if sig is None:
            sig = eng != "pe"
        deps = self._deps(reads, writes)
        idx = len(self.ops[eng])
        self.ops[eng].append(dict(fn=fn, deps=deps, sig=sig, dma=None))
        tok = ("e", eng, idx)
        self._mark(reads, writes, tok, eng)

    def dma(self, q, fn, sem, reads=(), writes=(), batch=False, inc=16):
        if self.dry:
            return
        deps = self._deps(reads, writes)
        self.dma_cnt[sem] = self.dma_cnt.get(sem, 0) + inc
        tok = ("d", sem, None if batch else self.dma_cnt[sem])
        self.ops[q].append(dict(fn=fn, deps=deps, sig=False, dma=sem, inc=inc))
        self._mark(reads, writes, tok, "dma:" + sem)
        return tok

    def finalize(self, stack, final_tokens):
        nc = self.nc
        for e in self.ENGS:
            for op in self.ops[e]:
                op["sig"] = False
        for e in self.ENGS:
            for idx, op in enumerate(self.ops[e]):
                for d in op["deps"]:
                    if d[0] == "e":
                        _, de, di = d
                        if de == e and (e == "pe" or di >= idx):
                            continue
                        self.ops[de][di]["sig"] = True
        sigval = {}
        for e in self.ENGS:
            c = 0
            vals = []
            for op in self.ops[e]:
                if op["sig"]:
                    c += 1
                vals.append(c)
            nxt = [None] * len(vals)
            cur = None
            for i in range(len(vals) - 1, -1, -1):
                if self.ops[e][i]["sig"]:
                    cur = vals[i]
                nxt[i] = cur
            sigval[e] = nxt
        esem = {e: stack.enter_context(nc.semaphore("s_" + e)) for e in self.ENGS if e != "sp"}
        dsem = {k: stack.enter_context(nc.semaphore("d_" + k)) for k in self.dma_cnt}
        block = stack.enter_context(nc.Block())
        handles = {"pe": "tensor", "act": "scalar", "dve": "vector", "pool": "gpsimd", "sp": "sync"}

        def emit(e, eng):
            waited = {}
            for idx, op in enumerate(self.ops[e]):
                need = {}
                for d in op["deps"]:
                    if d[0] == "e":
                        _, de, di = d
                        if de == e and e == "pe":
                            continue
                        if de == e and di >= idx:
                            continue
                        v = sigval[de][di]
                        assert v is not None, (de, di)
                        key = ("e", de)
                    else:
                        _, sk, v = d
                        if v is None:
                            v = self.dma_cnt[sk]
                        key = ("d", sk)
                    if v > need.get(key, 0):
                        need[key] = v
                for key, v in need.items():
                    if waited.get(key, 0) >= v:
                        continue
                    waited[key] = v
                    s = esem[key[1]] if key[0] == "e" else dsem[key[1]]
                    eng.wait_ge(s, v)
                ins = op["fn"](eng)
                if op["dma"] is not None:
                    ins.then_inc(dsem[op["dma"]], op["inc"])
                elif op["sig"]:
                    ins.then_inc(esem[e], 1)
            if e == "sp":
                for tok in final_tokens:
                    eng.wait_ge(dsem[tok[1]], tok[2] if tok[2] is not None else self.dma_cnt[tok[1]])

        @block.tensor
        def _(eng):
            emit("pe", eng)

        @block.scalar
        def _(eng):
            emit("act", eng)

        @block.vector
        def _(eng):
            emit("dve", eng)

        @block.gpsimd
        def _(eng):
            emit("pool", eng)

        @block.sync
        def _(eng):
            emit("sp", eng)


class WStream:
    def __init__(self, p, slots, plan=None):
        self.p = p
        self.slots = slots
        self.plan = plan
        self.req = []
        self.n_used = 0
        self.n_issued = 0

    def _issue(self, j):
        src, n = self.plan[j]
        s = j % len(self.slots)
        slot = self.slots[s]
        dst = slot[:, 0:n]
        self.p.dma("pool", lambda e, dst=dst, src=src: e.dma_start(out=dst, in_=src, max_dma_last_dim=2048),
                   sem="w%d" % s, writes=[("w", s)])

    def use(self, src, ncols, hold=2):
        j = self.n_used
        self.n_used += 1
        if self.p.dry:
            self.req.append((src, ncols))
            return self.slots[j % len(self.slots)], ("w", j % len(self.slots))
        ahead = min(len(self.plan), j + len(self.slots) - hold + 1)
        while self.n_issued < ahead:
            self._issue(self.n_issued)
            self.n_issued += 1
        s = j % len(self.slots)
        return self.slots[s], ("w", s)


class Ctx:
    pass


def rmsnorm(c, src, srckey, ntok, gain_col, out, outkey):
    p = c.p
    nth = (ntok + TH - 1) // TH
    for t in range(nth):
        n = min(TH, ntok - t * TH)
        ts = slice(t * TH, t * TH + n)
        ps = c.bank()
        for kc in range(KC):
            sq = c.sq[kc % 2]
            p.add("act", lambda e, sq=sq, kc=kc, ts=ts, n=n: e.activation(out=sq[:, 0:n], in_=src[:, kc, ts], func=AF.Square),
                  reads=[srckey(kc, t)], writes=[("sq", kc % 2)])
            p.add("pe", lambda e, ps=ps, sq=sq, kc=kc, n=n: e.matmul(ps[0][:, 0:n], lhsT=c.ones_bf[:], rhs=sq[:, 0:n],
                                                                      start=(kc == 0), stop=(kc == KC - 1)),
                  reads=[("sq", kc % 2), "ones"], writes=[ps[1]])
        p.add("act", lambda e, ps=ps, n=n: e.activation(out=c.rstd[:, 0:n], in_=ps[0][:, 0:n], func=AF.Sqrt, scale=1.0 / D,
                                                       bias=c.eps_col[:, 0:1]),
              reads=[ps[1], "eps"], writes=["rstd"])
        p.add("dve", lambda e, n=n: e.reciprocal(out=c.rstd[:, 0:n], in_=c.rstd[:, 0:n]), reads=["rstd"], writes=["rstd"])
        for kc in range(KC):
            p.add("dve", lambda e, kc=kc, ts=ts, n=n: e.scalar_tensor_tensor(
                out=out[:, kc, ts], in0=src[:, kc, ts], scalar=c.gain_all[:, gain_col + kc:gain_col + kc + 1],
                in1=c.rstd[:, 0:n], op0=ALU.mult, op1=ALU.mult),
                reads=[srckey(kc, t), "rstd", "gains"], writes=[outkey(kc, t)])


def xkey(kc, t):
    return ("x", kc, t)


def hkey(kc, t):
    return ("h", t)


def fence(c):
    c.p.add("dve", lambda e: e.memset(c.dummy[:], 0.0), writes=["arena", "dummy"])


def ffn(c, li, which):
    p = c.p
    fence(c)
    actT = c.arena_bf(0, FPG * T).rearrange("p (a b) -> p a b", a=FPG)
    rmsnorm(c, c.xT, xkey, T, c.gains[("ffn%d" % which, li)], c.hT, hkey)
    wgu = c.dram["wgu%d" % which]
    wdn = c.dram["wdn%d" % which]
    for g in range(FF_GROUPS):
        for fi in range(FPG):
            f = g * FPG + fi
            wg, wgk = c.ws.use(wgu[li, f], KC * 128)
            wu, wuk = c.ws.use(wgu[li, NFF + f], KC * 128)
            for t in range(NTH):
                ts = slice(t * TH, (t + 1) * TH)
                pg = c.bank()
                pu = c.bank()
                for kc in range(KC):
                    p.add("pe", lambda e, pg=pg, wg=wg, kc=kc, ts=ts: e.matmul(
                        pg[0][:], lhsT=wg[:, kc * 128:(kc + 1) * 128], rhs=c.hT[:, kc, ts],
                        start=(kc == 0), stop=(kc == KC - 1)),
                        reads=[wgk, ("h", t)], writes=[pg[1]])
                for kc in range(KC):
                    p.add("pe", lambda e, pu=pu, wu=wu, kc=kc, ts=ts: e.matmul(
                        pu[0][:], lhsT=wu[:, kc * 128:(kc + 1) * 128], rhs=c.hT[:, kc, ts],
                        start=(kc == 0), stop=(kc == KC - 1)),
                        reads=[wuk, ("h", t)], writes=[pu[1]])
                sg, sgk = c.tmpf()
                p.add("act", lambda e, sg=sg, pg=pg: e.activation(out=sg[:], in_=pg[0][:], func=AF.Silu),
                      reads=[pg[1]], writes=[sgk])
                p.add("dve", lambda e, sg=sg, pu=pu, fi=fi, ts=ts: e.tensor_tensor(
                    out=actT[:, fi, ts], in0=pu[0][:], in1=sg[:], op=ALU.mult),
                    reads=[pu[1], sgk, "arena"], writes=[("act", t)])
        for m in range(KC):
            wd, wdk = c.ws.use(wdn[li, g, m], FPG * 128)
            for t in range(NTH):
                ts = slice(t * TH, (t + 1) * TH)
                py = c.bank()
                for fi in range(FPG):
                    p.add("pe", lambda e, py=py, wd=wd, fi=fi, ts=ts: e.matmul(
                        py[0][:], lhsT=wd[:, fi * 128:(fi + 1) * 128], rhs=actT[:, fi, ts],
                        start=(fi == 0), stop=(fi == FPG - 1)),
                        reads=[wdk, ("act", t), "arena"], writes=[py[1]])
                p.add("dve", lambda e, py=py, m=m, ts=ts: e.scalar_tensor_tensor(
                    out=c.xT[:, m, ts], in0=py[0][:], scalar=0.5, in1=c.xT[:, m, ts],
                    op0=ALU.mult, op1=ALU.add),
                    reads=[py[1], ("x", m, t)], writes=[("x", m, t)])


XH = 4
NMEM = 256


def xattn(c, li):
    p = c.p
    fence(c)
    o = 0
    memT = c.arena_f(o, KC * NMEM).rearrange("p (a b) -> p a b", a=KC); o += KC * NMEM
    memn = c.arena_bf(o, KC * NMEM).rearrange("p (a b) -> p a b", a=KC); o += KC * NMEM // 2
    kT = c.arena_bf(o, XH * NMEM).rearrange("p (a b) -> p a b", a=XH); o += XH * NMEM // 2
    vv = c.arena_bf(o, 2 * 512).rearrange("p (a b) -> p a b", a=2); o += 512
    qT = c.arena_bf(o, XH * T).rearrange("p (a b) -> p a b", a=XH); o += XH * T // 2
    oT = c.arena_bf(o, XH * T).rearrange("p (a b) -> p a b", a=XH); o += XH * T // 2
    pT = [c.arena_bf(o + i * 256, 512) for i in range(4)]; o += 4 * 256
    A = "arena"
    for kc in range(KC):
        p.dma("sp", lambda e, kc=kc: e.dma_start(out=memT[:, kc, :], in_=c.dram["memT"][kc, :, :]),
              sem="memin%d" % li, reads=[A], writes=[("memT", kc)], batch=True)
    rmsnorm(c, memT, lambda kc, t: ("memT", kc), NMEM, c.gains[("mem", li)], memn, lambda kc, t: "memn")
    for h in range(XH):
        w, wk = c.ws.use(c.dram["wxk"][li, h], KC * 128)
        ps = c.bank()
        for kc in range(KC):
            p.add("pe", lambda e, ps=ps, w=w, kc=kc: e.matmul(ps[0][:, 0:NMEM], lhsT=w[:, kc * 128:(kc + 1) * 128], rhs=memn[:, kc, :],
                                                             start=(kc == 0), stop=(kc == KC - 1)),
                  reads=[wk, "memn", A], writes=[ps[1]])
        p.add("act", lambda e, ps=ps, h=h: e.copy(out=kT[:, h, :], in_=ps[0][:, 0:NMEM]), reads=[ps[1], A], writes=["kT"])
    pv = [c.bank(), c.bank()]
    nq = KC // 4
    for q in range(nq):
        w, wk = c.ws.use(c.dram["wxv"][li, q], 4 * 512)
        for k4 in range(4):
            kc = q * 4 + k4
            for mt in range(2):
                p.add("pe", lambda e, ps=pv[mt], w=w, k4=k4, kc=kc, mt=mt: e.matmul(
                    ps[0][:], lhsT=memn[:, kc, mt * 128:(mt + 1) * 128], rhs=w[:, k4 * 512:(k4 + 1) * 512],
                    start=(kc == 0), stop=(kc == KC - 1)),
                    reads=[wk, "memn", A], writes=[pv[mt][1]])
    for mt in range(2):
        p.add("act", lambda e, mt=mt: e.copy(out=vv[:, mt, :], in_=pv[mt][0][:]), reads=[pv[mt][1], A], writes=["vv"])
    rmsnorm(c, c.xT, xkey, T, c.gains[("xq", li)], c.hT, hkey)
    for h in range(XH):
        w, wk = c.ws.use(c.dram["wxq"][li, h], KC * 128)
        for t in range(NTH):
            ts = slice(t * TH, (t + 1) * TH)
            ps = c.bank()
            for kc in range(KC):
                p.add("pe", lambda e, ps=ps, w=w, kc=kc, ts=ts: e.matmul(ps[0][:], lhsT=w[:, kc * 128:(kc + 1) * 128], rhs=c.hT[:, kc, ts],
                                                                         start=(kc == 0), stop=(kc == KC - 1)),
                      reads=[wk, ("h", t)], writes=[ps[1]])
            p.add("act", lambda e, ps=ps, h=h, ts=ts: e.copy(out=qT[:, h, ts], in_=ps[0][:]), reads=[ps[1], A], writes=[("qT", h, t)])
    pi = 0
    for h in range(XH):
        for t in range(NTH):
            ts = slice(t * TH, (t + 1) * TH)
            pts = []
            for mt in range(2):
                ps = c.bank()
                p.add("pe", lambda e, ps=ps, h=h, mt=mt, ts=ts: e.matmul(ps[0][:], lhsT=kT[:, h, mt * 128:(mt + 1) * 128], rhs=qT[:, h, ts],
                                                                         start=True, stop=True),
                      reads=["kT", ("qT", h, t), A], writes=[ps[1]])
                pt = pT[pi % 4]; ptk = ("pT", pi % 4); pi += 1
                p.add("act", lambda e, ps=ps, pt=pt: e.activation(out=pt[:], in_=ps[0][:], func=AF.Exp, scale=128 ** -0.5),
                      reads=[ps[1], A], writes=[ptk])
                pts.append((pt, ptk))
            po = c.bank()
            pd = c.bank()
            for mt in range(2):
                p.add("pe", lambda e, po=po, h=h, mt=mt, pt=pts[mt][0]: e.matmul(po[0][:], lhsT=vv[:, mt, h * 128:(h + 1) * 128], rhs=pt[:],
                                                                                 start=(mt == 0), stop=(mt == 1)),
                      reads=["vv", pts[mt][1], A], writes=[po[1]])
            for mt in range(2):
                p.add("pe", lambda e, pd=pd, mt=mt, pt=pts[mt][0]: e.matmul(pd[0][:], lhsT=c.ones_bf[:], rhs=pt[:],
                                                                            start=(mt == 0), stop=(mt == 1)),
                      reads=["ones", pts[mt][1], A], writes=[pd[1]])
            rd, rdk = c.tmpf()
            p.add("dve", lambda e, rd=rd, pd=pd: e.reciprocal(out=rd[:], in_=pd[0][:]), reads=[pd[1]], writes=[rdk])
            p.add("dve", lambda e, rd=rd, po=po, h=h, ts=ts: e.tensor_tensor(out=oT[:, h, ts], in0=po[0][:], in1=rd[:], op=ALU.mult),
                  reads=[po[1], rdk, A], writes=[("oT", h, t)])
    for mg in range(KC // 4):
        w, wk = c.ws.use(c.dram["wxo"][li, mg], 4 * XH * 128)
        for m4 in range(4):
            m = mg * 4 + m4
            for t in range(NTH):
                ts = slice(t * TH, (t + 1) * TH)
                ps = c.bank()
                for h in range(XH):
                    p.add("pe", lambda e, ps=ps, w=w, m4=m4, h=h, ts=ts: e.matmul(
                        ps[0][:], lhsT=w[:, (m4 * XH + h) * 128:(m4 * XH + h + 1) * 128], rhs=oT[:, h, ts],
                        start=(h == 0), stop=(h == XH - 1)),
                        reads=[wk, ("oT", h, t), A], writes=[ps[1]])
                p.add("dve", lambda e, ps=ps, m=m, ts=ts: e.tensor_tensor(out=c.xT[:, m, ts], in0=ps[0][:], in1=c.xT[:, m, ts], op=ALU.add),
                      reads=[ps[1], ("x", m, t)], writes=[("x", m, t)])


DBG = None
HD = 64
NKV = 4
NBLK = T // 128


def cc_allgather(c, src_sb, ncols, key):
    p = c.p
    cin = c.dram["cin_" + key]
    cout = c.dram["cout_" + key]
    p.dma("sp", lambda e: e.dma_start(out=cin[:, :], in_=src_sb), sem="ccin_" + key, reads=[("ccsrc", key), "arena"],
          writes=[("cin", key)])
    p.dma("pool", lambda e: e.collective_compute("AllGather", ALU.bypass, replica_groups=[list(range(NCORES))],
                                                  ins=[cin[:, :]], outs=[cout[:, :]]),
          sem="cc_" + key, reads=[("cin", key)], writes=[("cout", key)], inc=1)
    return cout


def select_prev(c, cout, ncols, dst_fn, key, chunk=128):
    p = c.p
    A = "arena"
    view = cout.rearrange("(r q) n -> q r n", q=128)
    for ci in range(ncols // chunk):
        st, stk = c.stage[ci % 2], ("stage", ci % 2)
        p.dma("sp", lambda e, st=st, ci=ci: e.dma_start(out=st[:, :, :], in_=view[:, :, ci * chunk:(ci + 1) * chunk]),
              sem="stg%d" % (ci % 2), reads=[("cout", key), A], writes=[stk])
        acc, acck = c.tmpf()
        p.add("dve", lambda e, st=st, acc=acc: e.tensor_scalar(out=acc[:, 0:chunk], in0=st[:, 0, :], scalar1=c.sel[:, 0:1], scalar2=None,
                                                                 op0=ALU.mult),
              reads=[stk, "sel", A], writes=[acck])
        for r in range(1, NCORES):
            last = r == NCORES - 1
            dst, dk = dst_fn(ci) if last else (acc[:, 0:chunk], acck)
            p.add("dve", lambda e, st=st, acc=acc, r=r, dst=dst: e.scalar_tensor_tensor(
                out=dst, in0=st[:, r, :], scalar=c.sel[:, r:r + 1], in1=acc[:, 0:chunk], op0=ALU.mult, op1=ALU.add),
                reads=[stk, "sel", acck, A], writes=[dk])


def swa(c, li):
    p = c.p
    A = "arena"
    fence(c)
    REP = (D // HD) // NKV
    o = 0
    SIN = c.arena_f(o, T); o += T
    COS = c.arena_f(o, T); o += T
    K2 = c.arena_bf(o, NKV * (T + 128)).rearrange("p (a b) -> p a b", a=NKV); o += NKV * (T + 128) // 2
    V2 = c.arena_bf(o, (NBLK + 1) * 512).rearrange("p (a b) -> p a b", a=NBLK + 1); o += (NBLK + 1) * 256
    oT = c.arena_bf(o, KC * TH).rearrange("p (a b) -> p a b", a=KC); o += KC * TH // 2
    qA = [c.arena_bf(o + i * (TH // 2), TH) for i in range(2)]; o += TH
    qB = [c.arena_bf(o + i * (TH // 2), TH) for i in range(2)]; o += TH
    esink = c.arena_f(o, D // HD); o += D // HD
    tA = c.arena_f(o, T)
    tB = c.arena_f(o + T, T)
    tI = c.arena_f(o + 2 * T, T).bitcast(I32)
    hs = c.arena_f(o, 1024)
    c.stage = [c.arena_f(o + 1024 + i * 1024, 1024).rearrange("p (a b) -> p a b", a=NCORES) for i in range(2)]
    o += max(3 * T, 3072)
    assert o <= ARENA_F32, o
    rmsnorm(c, c.xT, xkey, T, c.gains[("mix", 1)], c.hT, hkey)
    for i in range(2):
        p.add("dve", lambda e, i=i: e.memset(qA[i][:, :], 0.0), reads=[A], writes=[("qt", i)])
        p.add("dve", lambda e, i=i: e.memset(qB[i][:, :], 0.0), reads=[A], writes=[("qt", i)])
    cst = c.swac
    p.dma("sp", lambda e: e.dma_start(out=tI, in_=c.dram["posi"][:, :]), sem="posi", reads=[A], writes=["tI"])
    p.add("dve", lambda e: e.tensor_copy(out=tA, in_=tI), reads=["tI", "swac", A], writes=["tA"])
    p.add("dve", lambda e: e.tensor_scalar(out=tA, in0=tA, scalar1=cst[:, 0:1], scalar2=None, op0=ALU.mult),
          reads=["tA", "swac"], writes=["tA"])

    def reduce_sin(dst, dkey, shift):
        src = tA
        if shift:
            p.add("dve", lambda e: e.tensor_scalar(out=tB, in0=tA, scalar1=0.25, scalar2=None, op0=ALU.add),
                  reads=["tA", A], writes=["tB"])
            src = tB
        sk = "tB" if shift else "tA"
        p.add("dve", lambda e: e.tensor_copy(out=tI, in_=src), reads=[sk, A], writes=["tI"])
        p.add("dve", lambda e: e.tensor_copy(out=dst, in_=tI), reads=["tI", A], writes=[dkey])
        p.add("dve", lambda e: e.tensor_tensor(out=dst, in0=src, in1=dst, op=ALU.subtract), reads=[sk, dkey], writes=[dkey])
        p.add("dve", lambda e: e.tensor_scalar(out=tI.bitcast(F32), in0=dst, scalar1=0.5, scalar2=None, op0=ALU.is_gt),
              reads=[dkey], writes=["tI"])
        p.add("dve", lambda e: e.tensor_tensor(out=dst, in0=dst, in1=tI.bitcast(F32), op=ALU.subtract), reads=[dkey, "tI"], writes=[dkey])
        p.add("dve", lambda e: e.tensor_scalar(out=tI.bitcast(F32), in0=dst, scalar1=-0.5, scalar2=None, op0=ALU.is_lt),
              reads=[dkey], writes=["tI"])
        p.add("dve", lambda e: e.tensor_tensor(out=dst, in0=dst, in1=tI.bitcast(F32), op=ALU.add), reads=[dkey, "tI"], writes=[dkey])
        p.add("act", lambda e: e.activation(out=dst, in_=dst, func=AF.Sin, scale=2.0 * np.pi), reads=[dkey], writes=[dkey])
    reduce_sin(SIN, "SIN", False)
    reduce_sin(COS, "COS", True)
    p.add("act", lambda e: e.activation(out=esink, in_=c.sinkb[:], func=AF.Exp), reads=["swac", A], writes=["esink"])
    if DBG == "tables":
        p.add("dve", lambda e: e.tensor_copy(out=c.xT[:, 0, :], in_=SIN), reads=["SIN", A, ("x", 0, 0), ("x", 0, 1)], writes=[("x", 0, 0), ("x", 0, 1)])
        p.add("dve", lambda e: e.tensor_copy(out=c.xT[:, 1, :], in_=COS), reads=["COS", A, ("x", 1, 0), ("x", 1, 1)], writes=[("x", 1, 0), ("x", 1, 1)])
        return

    def rope_evac(ps, bias_col, dst, dkey, ts, t):
        raw, rk = c.tmpf()
        p.add("act", lambda e, ps=ps, raw=raw: e.activation(out=raw[:], in_=ps[0][:], func=AF.Identity, bias=bias_col),
              reads=[ps[1], "swac"], writes=[rk])
        rb, rbk = c.sq[t % 2], ("sq", t % 2)
        p.add("act", lambda e, raw=raw, rb=rb: e.copy(out=rb[:], in_=raw[:]), reads=[rk], writes=[rbk])
        pr = c.bank()
        p.add("pe", lambda e, pr=pr, rb=rb: e.matmul(pr[0][:], lhsT=c.rotT[:], rhs=rb[:], start=True, stop=True),
              reads=[rbk, "swac"], writes=[pr[1]])
        p.add("dve", lambda e, raw=raw, ts=ts: e.tensor_tensor(out=raw[:], in0=raw[:], in1=COS[:, ts], op=ALU.mult),
              reads=[rk, "COS"], writes=[rk])
        t2, t2k = c.tmpf()
        p.add("dve", lambda e, pr=pr, t2=t2, ts=ts: e.tensor_tensor(out=t2[:], in0=pr[0][:], in1=SIN[:, ts], op=ALU.mult),
              reads=[pr[1], "SIN"], writes=[t2k])
        if isinstance(dst, tuple):
            for hh_ in range(2):
                rows_ = slice(hh_ * 64, (hh_ + 1) * 64)
                p.add("dve", lambda e, raw=raw, t2=t2, d_=dst[hh_], rows_=rows_: e.tensor_tensor(out=d_[rows_, :], in0=raw[rows_, :], in1=t2[rows_, :], op=ALU.add),
                      reads=[rk, t2k, A, dkey], writes=[dkey])
        else:
            p.add("dve", lambda e, raw=raw, t2=t2, dst=dst: e.tensor_tensor(out=dst, in0=raw[:], in1=t2[:], op=ALU.add),
                  reads=[rk, t2k, A], writes=[dkey])

    for g in range(NKV):
        w, wk = c.ws.use(c.dram["wk2"][li, g], KC * 128)
        for t in range(NTH):
            ts = slice(t * TH, (t + 1) * TH)
            ps = c.bank()
            for kc in range(KC):
                p.add("pe", lambda e, ps=ps, w=w, kc=kc, ts=ts: e.matmul(ps[0][:], lhsT=w[:, kc * 128:(kc + 1) * 128], rhs=c.hT[:, kc, ts],
                                                                         start=(kc == 0), stop=(kc == KC - 1)),
                      reads=[wk, ("h", t)], writes=[ps[1]])
            rope_evac(ps, c.bk2[:, g:g + 1], K2[:, g, 128 + t * TH:128 + (t + 1) * TH], ("K2", g, t), ts, t)
    for vp in range(NBLK // 4):
        pvs = [c.bank() for _ in range(4)]
        for q in range(KC // 4):
            w, wk = c.ws.use(c.dram["wv2"][li, q], 4 * 512)
            for k4 in range(4):
                kc = q * 4 + k4
                for t4 in range(4):
                    tb = vp * 4 + t4
                    p.add("pe", lambda e, ps=pvs[t4], w=w, k4=k4, kc=kc, tb=tb: e.matmul(
                        ps[0][:], lhsT=c.hT[:, kc, tb * 128:(tb + 1) * 128], rhs=w[:, k4 * 512:(k4 + 1) * 512],
                        start=(kc == 0), stop=(kc == KC - 1)),
                        reads=[wk, ("h", tb // (TH // 128))], writes=[pvs[t4][1]])
        for t4 in range(4):
            tb = vp * 4 + t4
            p.add("dve", lambda e, tb=tb, ps=pvs[t4]: e.tensor_tensor(out=V2[:, 1 + tb, :], in0=ps[0][:], in1=c.bv2[:], op=ALU.add),
                  reads=[pvs[t4][1], "swac", A], writes=[("V2", 1 + tb)])
    if DBG == "proj":
        return
    fence(c)
    for g in range(NKV):
        p.add("act", lambda e, g=g: e.copy(out=hs[:, g * 128:(g + 1) * 128], in_=K2[:, g, T:T + 128]),
              reads=[("K2", g, NTH - 1), A], writes=[("ccsrc", "swa")])
    p.add("act", lambda e: e.copy(out=hs[:, 512:1024], in_=V2[:, NBLK, :]), reads=[("V2", NBLK), A], writes=[("ccsrc", "swa")])
    cout = cc_allgather(c, hs, 1024, "swa")

    def dstf(ci):
        if ci < NKV:
            return K2[:, ci, 0:128], ("K2h", ci)
        return V2[:, 0, (ci - NKV) * 128:(ci - NKV + 1) * 128], ("V2", 0)
    if DBG == "cc":
        return
    select_prev(c, cout, 1024, dstf, "swa")
    if DBG == "halo":
        return
    mi = 0
    BPH = TH // 128
    for half in range(NTH):
        ts = slice(half * TH, (half + 1) * TH)
        for j in range(KC):
            g = (2 * j) // REP
            w, wk = c.ws.use(c.dram["wqt"][li, j], KC * 128)
            qk = ("qt", j % 2)
            qab = (qA[j % 2], qB[j % 2])
            ps = c.bank()
            for kc in range(KC):
                p.add("pe", lambda e, ps=ps, w=w, kc=kc, ts=ts: e.matmul(ps[0][:], lhsT=w[:, kc * 128:(kc + 1) * 128], rhs=c.hT[:, kc, ts],
                                                                         start=(kc == 0), stop=(kc == KC - 1)),
                      reads=[wk, ("h", half)], writes=[ps[1]])
            rope_evac(ps, c.bq[:, j:j + 1], qab, qk, ts, half)
            for ql in range(BPH):
                qb = half * BPH + ql
                qs = slice(ql * 128, (ql + 1) * 128)
                pts = []
                for kb in range(2):
                    ks = slice((qb + kb) * 128, (qb + kb + 1) * 128)
                    kkeys = [("K2h", g)] if (qb + kb) == 0 else [("K2", g, (qb + kb - 1) // BPH)]
                    ps = c.bank()
                    for hh in range(2):
                        p.add("pe", lambda e, ps=ps, hh=hh, ks=ks, qs=qs, qz=qab[hh], g=g: e.matmul(
                            ps[0][:, hh * 128:(hh + 1) * 128], lhsT=K2[:, g, ks], rhs=qz[:, qs], start=True, stop=True),
                            reads=kkeys + [qk, A], writes=[ps[1]])
                    ex, exk = c.tmpf()
                    p.add("act", lambda e, ps=ps, ex=ex: e.activation(out=ex[:, 0:256], in_=ps[0][:, 0:256], func=AF.Exp, scale=HD ** -0.5),
                          reads=[ps[1]], writes=[exk])
                    mk = c.masks[:, 256:512] if kb == 1 else (c.masks[:, 512:768] if qb == 0 else c.masks[:, 0:256])
                    pt, ptk = c.pts[mi % 4], ("pts", mi % 4)
                    mi += 1
                    p.add("dve", lambda e, ex=ex, mk=mk, pt=pt: e.tensor_tensor(out=pt[:], in0=ex[:, 0:256], in1=mk, op=ALU.mult),
                          reads=[exk, "swac"], writes=[ptk])
                    pts.append((pt, ptk))
                po = c.bank()
                pd = c.bank()
                for kb in range(2):
                    p.add("pe", lambda e, po=po, kb=kb, qb=qb, g=g, pt=pts[kb][0]: e.matmul(
                        po[0][:, 0:256], lhsT=V2[:, qb + kb, g * 128:(g + 1) * 128], rhs=pt[:], start=(kb == 0), stop=(kb == 1)),
                        reads=[("V2", qb + kb), pts[kb][1], A], writes=[po[1]])
                for kb in range(2):
                    p.add("pe", lambda e, pd=pd, kb=kb, pt=pts[kb][0]: e.matmul(pd[0][:, 0:256], lhsT=c.ones_bf[:], rhs=pt[:],
                                                                                start=(kb == 0), stop=(kb == 1)),
                          reads=["ones", pts[kb][1]], writes=[pd[1]])
                rd, rdk = c.tmpf()
                for hh in range(2):
                    p.add("dve", lambda e, rd=rd, pd=pd, hh=hh, j=j: e.tensor_scalar(
                        out=rd[:, hh * 128:(hh + 1) * 128], in0=pd[0][:, hh * 128:(hh + 1) * 128],
                        scalar1=esink[:, 2 * j + hh:2 * j + hh + 1], scalar2=None, op0=ALU.add),
                        reads=[pd[1], "esink"], writes=[rdk])
                p.add("dve", lambda e, rd=rd: e.reciprocal(out=rd[:, 0:256], in_=rd[:, 0:256]), reads=[rdk], writes=[rdk])
                for hh in range(2):
                    rows = slice(hh * 64, (hh + 1) * 64)
                    p.add("dve", lambda e, rd=rd, po=po, hh=hh, rows=rows, j=j, qs=qs: e.tensor_tensor(
                        out=oT[rows, j, qs], in0=po[0][rows, hh * 128:(hh + 1) * 128], in1=rd[rows, hh * 128:(hh + 1) * 128], op=ALU.mult),
                        reads=[po[1], rdk, A], writes=[("oT", j)])
        for m in range(KC):
            w, wk = c.ws.use(c.dram["wo"][li, m], KC * 128)
            ps = c.bank()
            for kc in range(KC):
                p.add("pe", lambda e, ps=ps, w=w, kc=kc: e.matmul(ps[0][:], lhsT=w[:, kc * 128:(kc + 1) * 128], rhs=oT[:, kc, :],
                                                                  start=(kc == 0), stop=(kc == KC - 1)),
                      reads=[wk, ("oT", kc), A], writes=[ps[1]])
            p.add("dve", lambda e, ps=ps, m=m, ts=ts, half=half: e.tensor_tensor(out=c.xT[:, m, ts], in0=ps[0][:], in1=c.xT[:, m, ts], op=ALU.add),
                  reads=[ps[1], ("x", m, half)], writes=[("x", m, half)])


NST = 128
SSD_G = 4


def even_mixer(c):
    p = c.p
    A = "arena"
    H = D // 64
    HPG = H // SSD_G
    GW = D // 4
    CD = D + 2 * SSD_G * NST
    NT = CD // 128
    NXT = D // 128
    KCS = min(KC, WSLOT // GW)
    NTG = KC // KCS
    HB = NBLK // 2
    fence(c)
    rmsnorm(c, c.xT, xkey, T, c.gains[("mix", 0)], c.hT, hkey)
    xsp = c.dram["xsp"]
    for kc in range(KC):
        p.dma("sp", lambda e, kc=kc: e.dma_start(out=xsp[kc, :, :], in_=c.xT[:, kc, :]), sem="xsp", batch=True,
              reads=[("x", kc, t) for t in range(NTH)], writes=[("xsp", kc)])
    XK = [("x", kc, t) for kc in range(KC) for t in range(NTH)]
    p.add("dve", lambda e: e.memset(c.dummy[:], 0.0), writes=XK + ["arena", "dummy"])
    need = NBLK * D // 2 + 4 * 2048
    if KC * T >= need:
        reg = lambda off, n: c.xT[:, :, :].rearrange("p a b -> p (a b)")[:, off:off + n]
        regbase = 0
        o2 = 0
    else:
        reg = c.arena_f
        o2 = ARENA_F32 - need
    regbf = lambda off, n: reg(off, (n + 1) // 2).bitcast(BF16)[:, 0:n]
    xs_tok = regbf(o2, NBLK * D).rearrange("p (a b) -> p a b", a=NBLK); o2 += NBLK * D // 2
    B_tok = regbf(o2, NBLK * 512).rearrange("p (a b) -> p a b", a=NBLK); o2 += NBLK * 256
    BT = regbf(o2, SSD_G * T).rearrange("p (a b) -> p a b", a=SSD_G); o2 += SSD_G * T // 2
    CT = regbf(o2, SSD_G * T).rearrange("p (a b) -> p a b", a=SSD_G); o2 += SSD_G * T // 2
    S = reg(o2, SSD_G * 512).rearrange("p (a b) -> p a b", a=SSD_G); o2 += SSD_G * 512
    o = 0
    mix = c.arena_bf(o, KC * TH).rearrange("p (a b) -> p a b", a=KC); o += KC * TH // 2
    dtv = c.arena_f(o, NBLK * H).rearrange("p (a b) -> p a b", a=NBLK); o += NBLK * H
    av = c.arena_f(o, NBLK * H).rearrange("p (a b) -> p a b", a=NBLK); o += NBLK * H
    acs = c.arena_f(o, NBLK * H).rearrange("p (a b) -> p a b", a=NBLK); o += NBLK * H
    nacs = c.arena_f(o, NBLK * H).rearrange("p (a b) -> p a b", a=NBLK); o += NBLK * H
    wst = c.arena_f(o, NBLK * H).rearrange("p (a b) -> p a b", a=NBLK); o += NBLK * H
    ea = c.arena_f(o, NBLK * H).rearrange("p (a b) -> p a b", a=NBLK); o += NBLK * H
    cdv = c.arena_f(o, NBLK * H).rearrange("p (a b) -> p a b", a=NBLK); o += NBLK * H
    tot = c.arena_f(o, NBLK * H).rearrange("p (a b) -> p a b", a=NBLK); o += NBLK * H
    Aneg = c.arena_f(o, H); o += H
    ltot = c.arena_f(o, H); o += H
    halo = c.arena_f(o, NT * 3).rearrange("p (a b) -> p a b", a=NT); o += NT * 3
    hsend = c.arena_f(o, 128); o += 128
    wsm = c.arena_bf(o, 4 * 128).rearrange("p (a b) -> p a b", a=4); o += 256
    sm = [c.arena_f(o + i * 64, 64) for i in range(4)]; o += 256
    lng = c.arena_f(o, GW); o += GW
    lnb = c.arena_f(o, GW); o += GW
    UW = max(2 * (T + 4), 2048 + 64)
    tf = [c.arena_f(o + i * (T + 4), T + 4) for i in range(2)]
    fx = c.arena_f(o, 2048 + 64)
    ybuf = [c.arena_f(o + i * 512, 512) for i in range(4)]
    o += UW
    tg = [c.arena_f(o + i * 512, 512) for i in range(4)]; o += 2048
    tb = [c.arena_bf(o + i * 64, 128) for i in range(4)]; o += 256
    vnb = c.arena_bf(o, HB * GW).rearrange("p (a b) -> p a b", a=HB); o += HB * GW // 2
    cbms = [c.arena_f(o + i * 128, 128) for i in range(2)]; o += 256
    xdts = [c.arena_bf(o + i * 256, 512) for i in range(2)]; o += 512
    xds = [c.arena_bf(o + i * 256, 512) for i in range(2)]; o += 512
    ring = {"cbm": 0, "xdt": 0, "xd": 0}
    Sbf = c.arena_bf(o, 512); o += 256
    assert o <= (ARENA_F32 if KC * T >= need else ARENA_F32 - need), (o, ARENA_F32)
    c.stage = [tf[0][:, 0:1024].rearrange("p (a b) -> p a b", a=NCORES), tf[1][:, 0:1024].rearrange("p (a b) -> p a b", a=NCORES)]
    ec = c.econ
    Umat, NEGm, ident, onesf = ec[:, 0:128], ec[:, 128:256], ec[:, 256:384], ec[:, 384:512]
    EC = "econ"
    tgi = [0]

    def TG():
        k = tgi[0] % 4
        tgi[0] += 1
        return tg[k], ("tg", k)
    tbi = [0]

    def TB():
        k = tbi[0] % 4
        tbi[0] += 1
        return tb[k], ("tb", k)
    wx = c.dram["wxbc"]
    last = slice(T - 128, T)
    p.add("dve", lambda e: e.memset(hsend, 0.0), reads=[A], writes=[("ccsrc", "cv")])
    for ct in range(NT):
        w, wk = c.ws.use(wx[ct], KC * 128)
        ps = c.bank()
        for kc in range(KC):
            p.add("pe", lambda e, ps=ps, w=w, kc=kc: e.matmul(ps[0][:, 0:128], lhsT=w[:, kc * 128:(kc + 1) * 128], rhs=c.hT[:, kc, last],
                                                             start=(kc == 0), stop=(kc == KC - 1)),
                  reads=[wk, ("h", NTH - 1)], writes=[ps[1]])
        p.add("act", lambda e, ps=ps, ct=ct: e.copy(out=hsend[:, ct * 3:ct * 3 + 3], in_=ps[0][:, 125:128]),
              reads=[ps[1], A], writes=[("ccsrc", "cv")])
    cout = cc_allgather(c, hsend, 128, "cv")
    st = c.stage[0]
    view = cout.rearrange("(r q) n -> q r n", q=128)
    p.dma("sp", lambda e: e.dma_start(out=st[:, :, :], in_=view[:, :, :]), sem="stg0", reads=[("cout", "cv"), A], writes=[("stage", 0)])
    hfl = halo.rearrange("p a b -> p (a b)")
    p.add("dve", lambda e: e.tensor_scalar(out=hfl, in0=st[:, 0, 0:NT * 3], scalar1=c.sel[:, 0:1], scalar2=None, op0=ALU.mult),
          reads=[("stage", 0), "sel", A], writes=["halo"])
    for r in range(1, NCORES):
        p.add("dve", lambda e, r=r: e.scalar_tensor_tensor(out=hfl, in0=st[:, r, 0:NT * 3], scalar=c.sel[:, r:r + 1], in1=hfl,
                                                           op0=ALU.mult, op1=ALU.add),
              reads=[("stage", 0), "sel", "halo"], writes=["halo"])
    if DBG == "e1":
        return
    p.add("act", lambda e: e.activation(out=Aneg, in_=c.hcon[:, H:2 * H], func=AF.Exp), reads=["evc", A], writes=["Aneg"])
    p.add("dve", lambda e: e.tensor_scalar(out=Aneg, in0=Aneg, scalar1=-1.0, scalar2=None, op0=ALU.mult), reads=["Aneg"], writes=["Aneg"])
    for g4 in range(4):
        p.add("dve", lambda e, g4=g4: e.tensor_tensor(out=wsm[:, g4, :], in0=c.wsT[:, g4 * 128:(g4 + 1) * 128], in1=Umat, op=ALU.mult),
              reads=["evc", EC, A], writes=["wsm"])
    wd, wdk = c.ws.use(c.dram["wdt"], KC * H)
    for b in range(NBLK):
        bs = slice(b * 128, (b + 1) * 128)
        ps = c.bank()
        for kc in range(KC):
            p.add("pe", lambda e, ps=ps, kc=kc, bs=bs: e.matmul(ps[0][:, 0:H], lhsT=c.hT[:, kc, bs], rhs=wd[:, kc * H:(kc + 1) * H],
                                                                start=(kc == 0), stop=(kc == KC - 1)),
                  reads=[wdk, ("h", b // HB)], writes=[ps[1]])
        p.add("dve", lambda e, ps=ps, b=b: e.tensor_tensor(out=dtv[:, b, :], in0=ps[0][:, 0:H], in1=c.hcon[:, 0:H], op=ALU.add),
              reads=[ps[1], "evc", A], writes=[("dtv", b)])
        p.add("act", lambda e, b=b: e.activation(out=dtv[:, b, :], in_=dtv[:, b, :], func=AF.Exp), reads=[("dtv", b)], writes=[("dtv", b)])
        p.add("act", lambda e, b=b: e.activation(out=dtv[:, b, :], in_=dtv[:, b, :], func=AF.Ln, bias=c.one_col[:, 0:1]),
              reads=[("dtv", b), "eps2"], writes=[("dtv", b)])
        p.add("dve", lambda e, b=b: e.tensor_tensor(out=av[:, b, :], in0=dtv[:, b, :], in1=Aneg, op=ALU.mult),
              reads=[("dtv", b), "Aneg"], writes=[("av", b)])
        pc = c.bank()
        p.add("pe", lambda e, pc=pc, b=b: e.matmul(pc[0][:, 0:H], lhsT=Umat, rhs=av[:, b, :], start=True, stop=True),
              reads=[("av", b), EC], writes=[pc[1]])
        pt_ = c.bank()
        p.add("pe", lambda e, pt_=pt_, b=b: e.matmul(pt_[0][:, 0:H], lhsT=onesf, rhs=av[:, b, :], start=True, stop=True),
              reads=[("av", b), EC], writes=[pt_[1]])
        p.add("act", lambda e, pc=pc, b=b: e.copy(out=acs[:, b, :], in_=pc[0][:, 0:H]), reads=[pc[1], A], writes=[("acs", b)])
        p.add("dve", lambda e, pc=pc, b=b: e.tensor_scalar(out=nacs[:, b, :], in0=pc[0][:, 0:H], scalar1=-1.0, scalar2=None, op0=ALU.mult),
              reads=[pc[1], A], writes=[("nacs", b)])
        p.add("act", lambda e, pt_=pt_, b=b: e.copy(out=tot[:, b, :], in_=pt_[0][:, 0:H]), reads=[pt_[1], A], writes=[("tot", b)])
        p.add("act", lambda e, pt_=pt_, b=b: e.activation(out=cdv[:, b, :], in_=pt_[0][:, 0:H], func=AF.Exp), reads=[pt_[1], A], writes=[("cdv", b)])
        p.add("act", lambda e, b=b: e.activation(out=ea[:, b, :], in_=acs[:, b, :], func=AF.Exp), reads=[("acs", b)], writes=[("ea", b)])
        p.add("dve", lambda e, b=b: e.tensor_tensor(out=wst[:, b, :], in0=tot[:, b, :], in1=acs[:, b, :], op=ALU.subtract),
              reads=[("tot", b), ("acs", b)], writes=[("wst", b)])
        p.add("act", lambda e, b=b: e.activation(out=wst[:, b, :], in_=wst[:, b, :], func=AF.Exp), reads=[("wst", b)], writes=[("wst", b)])
        p.add("dve", lambda e, b=b: e.tensor_tensor(out=wst[:, b, :], in0=wst[:, b, :], in1=dtv[:, b, :], op=ALU.mult),
              reads=[("wst", b), ("dtv", b)], writes=[("wst", b)])
        if b == 0:
            p.add("act", lambda e: e.copy(out=ltot, in_=tot[:, 0, :]), reads=[("tot", 0), A], writes=["ltot"])
        else:
            p.add("dve", lambda e, b=b: e.tensor_tensor(out=ltot, in0=ltot, in1=tot[:, b, :], op=ALU.add), reads=["ltot", ("tot", b)], writes=["ltot"])
    if DBG == "e2":
        return
    for ct in range(NT):
        w, wk = c.ws.use(wx[ct], KC * 128)
        pre, prk = tf[ct % 2], ("tf", ct % 2)
        p.add("act", lambda e, pre=pre, ct=ct: e.copy(out=pre[:, 0:3], in_=halo[:, ct, :]), reads=["halo", A], writes=[prk])
        for t in range(NTH):
            ts = slice(t * TH, (t + 1) * TH)
            ps = c.bank()
            for kc in range(KC):
                p.add("pe", lambda e, ps=ps, w=w, kc=kc, ts=ts: e.matmul(ps[0][:], lhsT=w[:, kc * 128:(kc + 1) * 128], rhs=c.hT[:, kc, ts],
                                                                         start=(kc == 0), stop=(kc == KC - 1)),
                      reads=[wk, ("h", t)], writes=[ps[1]])
            p.add("act", lambda e, ps=ps, pre=pre, t=t: e.copy(out=pre[:, 3 + t * TH:3 + (t + 1) * TH], in_=ps[0][:]), reads=[ps[1], A], writes=[prk])
        for t in range(NTH):
            cv, cvk = TG()
            p.add("dve", lambda e, cv=cv, pre=pre, ct=ct, t=t: e.tensor_scalar(
                out=cv[:], in0=pre[:, t * TH:t * TH + TH], scalar1=c.cw[:, ct * 4:ct * 4 + 1], scalar2=c.cb[:, ct:ct + 1],
                op0=ALU.mult, op1=ALU.add), reads=[prk, "evc"], writes=[cvk])
            for k in range(1, 4):
                p.add("dve", lambda e, cv=cv, pre=pre, ct=ct, t=t, k=k: e.scalar_tensor_tensor(
                    out=cv[:], in0=pre[:, t * TH + k:t * TH + k + TH], scalar=c.cw[:, ct * 4 + k:ct * 4 + k + 1], in1=cv[:],
                    op0=ALU.mult, op1=ALU.add), reads=[prk, "evc", cvk], writes=[cvk])
            if ct < NXT + SSD_G:
                cvb, cvbk = xdts[ring["xdt"] % 2], ("xdt", ring["xdt"] % 2)
                ring["xdt"] += 1
                p.add("act", lambda e, cv=cv, cvb=cvb: e.activation(out=cvb[:, 0:TH], in_=cv[:], func=AF.Silu), reads=[cvk, A], writes=[cvbk])
                if ct >= NXT:
                    g = ct - NXT
                    p.add("act", lambda e, cvb=cvb, g=g, t=t: e.copy(out=BT[:, g, t * TH:(t + 1) * TH], in_=cvb[:, 0:TH]), reads=[cvbk, A], writes=[("BT", g, t)])
                pt4 = c.bank()
                ptb = pt4[0].bitcast(BF16)
                for q4 in range(TH // 128):
                    p.add("pe", lambda e, ptb=ptb, cvb=cvb, q4=q4: e.transpose(ptb[:, q4 * 128:(q4 + 1) * 128], cvb[:, q4 * 128:(q4 + 1) * 128], c.identb[:]),
                          reads=[cvbk, "evc"], writes=[pt4[1]])
                for q4 in range(TH // 128):
                    b = t * (TH // 128) + q4
                    if ct < NXT:
                        dst, dk = xs_tok[:, b, ct * 128:(ct + 1) * 128], ("xs", b)
                    else:
                        dst, dk = B_tok[:, b, (ct - NXT) * 128:(ct - NXT + 1) * 128], ("Bt", b)
                    p.add("act", lambda e, ptb=ptb, q4=q4, dst=dst: e.copy(out=dst, in_=ptb[:, q4 * 128:(q4 + 1) * 128]),
                          reads=[pt4[1], A], writes=[dk])
            else:
                g = ct - NXT - SSD_G
                p.add("act", lambda e, cv=cv, g=g, t=t: e.activation(out=CT[:, g, t * TH:(t + 1) * TH], in_=cv[:], func=AF.Silu),
                      reads=[cvk, A], writes=[("CT", g, t)])

    if DBG == "e3":
        return
    def scale_heads(dst, b, g, wv, wkey, rk):
        for r in range(HPG):
            hh = g * HPG + r
            p.add("dve", lambda e, dst=dst, b=b, hh=hh, r=r, wv=wv: e.tensor_scalar(
                out=dst[:, r * 64:(r + 1) * 64], in0=xs_tok[:, b, hh * 64:(hh + 1) * 64], scalar1=wv[:, b, hh:hh + 1], scalar2=None,
                op0=ALU.mult), reads=[("xs", b), wkey, A], writes=[rk])

    def block_states(b, g):
        xd, xdk = xds[ring["xd"] % 2], ("xd", ring["xd"] % 2)
        ring["xd"] += 1
        scale_heads(xd, b, g, wst, ("wst", b), xdk)
        ps = c.bank()
        p.add("pe", lambda e, ps=ps, xd=xd, b=b, g=g: e.matmul(ps[0][:, 0:GW], lhsT=B_tok[:, b, g * 128:(g + 1) * 128], rhs=xd[:, 0:GW],
                                                               start=True, stop=True),
              reads=[("Bt", b), xdk, A], writes=[ps[1]])
        return ps

    def state_step(b, g, ps):
        for r in range(HPG):
            hh = g * HPG + r
            p.add("dve", lambda e, g=g, r=r, hh=hh, b=b, ps=ps: e.scalar_tensor_tensor(
                out=S[:, g, r * 64:(r + 1) * 64], in0=S[:, g, r * 64:(r + 1) * 64], scalar=cdv[:, b, hh:hh + 1],
                in1=ps[0][:, r * 64:(r + 1) * 64], op0=ALU.mult, op1=ALU.add),
                reads=[("S", g), ("cdv", b), ps[1], A], writes=[("S", g)])

    for g in range(SSD_G):
        p.add("dve", lambda e, g=g: e.memset(S[:, g, :], 0.0), reads=[A], writes=[("S", g)])
    for b in range(NBLK):
        for g in range(SSD_G):
            state_step(b, g, block_states(b, g))
    fence(c)
    fsend = fx
    p.add("dve", lambda e: e.memset(fx, 0.0), reads=[A], writes=[("ccsrc", "st")])
    for g in range(SSD_G):
        p.add("act", lambda e, g=g: e.copy(out=fsend[:, g * 512:g * 512 + GW], in_=S[:, g, 0:GW]), reads=[("S", g), A], writes=[("ccsrc", "st")])
    p.add("act", lambda e: e.copy(out=fsend[:, 2048:2048 + H], in_=ltot), reads=["ltot", A], writes=[("ccsrc", "st")])
    cout2 = cc_allgather(c, fsend, 2048 + 64, "st")
    for g in range(SSD_G):
        p.add("dve", lambda e, g=g: e.memset(S[:, g, :], 0.0), reads=[("S", g), ("ccsrc", "st")], writes=[("S", g)])
    view2 = cout2.rearrange("(r q) n -> q r n", q=128)
    for r in range(NCORES - 1):
        fr, frk = fx, ("ccsrc", "st")
        p.dma("sp", lambda e, fr=fr, r=r: e.dma_start(out=fr, in_=view2[:, r, :]), sem="frv",
              reads=[("cout", "st"), A], writes=[frk])
        dec, deck = sm[r % 4], ("sm", r % 4)
        p.add("act", lambda e, fr=fr, dec=dec: e.activation(out=dec[:, 0:H], in_=fr[:, 2048:2048 + H], func=AF.Exp), reads=[frk, A], writes=[deck])
        p.add("dve", lambda e, dec=dec: e.tensor_scalar(out=dec[:, 0:H], in0=dec[:, 0:H], scalar1=-1.0, scalar2=None, op0=ALU.add),
              reads=[deck], writes=[deck])
        p.add("dve", lambda e, dec=dec, r=r: e.tensor_scalar(out=dec[:, 0:H], in0=dec[:, 0:H], scalar1=c.ltm[:, r:r + 1], scalar2=1.0,
                                                            op0=ALU.mult, op1=ALU.add), reads=[deck, "econ"], writes=[deck])
        for g in range(SSD_G):
            for rr in range(HPG):
                hh = g * HPG + rr
                p.add("dve", lambda e, g=g, rr=rr, hh=hh, dec=dec: e.tensor_scalar(
                    out=S[:, g, rr * 64:(rr + 1) * 64], in0=S[:, g, rr * 64:(rr + 1) * 64], scalar1=dec[:, hh:hh + 1], scalar2=None,
                    op0=ALU.mult), reads=[("S", g), deck], writes=[("S", g)])
            p.add("dve", lambda e, g=g, fr=fr, r=r: e.scalar_tensor_tensor(
                out=S[:, g, 0:GW], in0=fr[:, g * 512:g * 512 + GW], scalar=c.ltm[:, r:r + 1], in1=S[:, g, 0:GW],
                op0=ALU.mult, op1=ALU.add), reads=[("S", g), frk, "econ"], writes=[("S", g)])

    if DBG == "e4":
        return
    fence(c)
    def tokmajor_proj(dname, grp, blocks, consume):
        tiles = [c.ws.use(c.dram[dname][grp, q], KCS * GW, hold=NTG + 1) for q in range(NTG)]
        for b in blocks:
            bs = slice(b * 128, (b + 1) * 128)
            ps = c.bank()
            for kc in range(KC):
                w, wk = tiles[kc // KCS]
                k4 = kc % KCS
                p.add("pe", lambda e, ps=ps, w=w, kc=kc, k4=k4, bs=bs: e.matmul(
                    ps[0][:, 0:GW], lhsT=c.hT[:, kc, bs], rhs=w[:, k4 * GW:(k4 + 1) * GW], start=(kc == 0), stop=(kc == KC - 1)),
                    reads=[wk, ("h", b // HB)], writes=[ps[1]])
            consume(b, ps)

    def transpose_to_mix(src, srck, mix, mkey, b, g, scale_col=None):
        pt4 = c.bank()
        ptb = pt4[0].bitcast(BF16)
        nq = GW // 128
        for q4 in range(nq):
            p.add("pe", lambda e, ptb=ptb, q4=q4, src=src: e.transpose(ptb[:, q4 * 128:(q4 + 1) * 128], src[:, q4 * 128:(q4 + 1) * 128], c.identb[:]),
                  reads=[srck, "evc"], writes=[pt4[1]])
        bl = b % HB
        for q4 in range(nq):
            kc = g * nq + q4
            dst = mix[:, kc, bl * 128:(bl + 1) * 128]
            if scale_col is None:
                p.add("act", lambda e, ptb=ptb, q4=q4, dst=dst: e.copy(out=dst, in_=ptb[:, q4 * 128:(q4 + 1) * 128]),
                      reads=[pt4[1], A], writes=[(mkey, kc)])
            else:
                p.add("dve", lambda e, ptb=ptb, q4=q4, dst=dst, kc=kc: e.tensor_scalar(
                    out=dst, in0=ptb[:, q4 * 128:(q4 + 1) * 128], scalar1=scale_col[:, kc:kc + 1], scalar2=None, op0=ALU.mult),
                    reads=[pt4[1], "evc", A], writes=[(mkey, kc)])

    def out_proj(part, half):
        for m in range(KC):
            w, wk = c.ws.use(c.dram["wout"][part, m], KC * 128)
            ps = c.bank()
            for kc in range(KC):
                p.add("pe", lambda e, ps=ps, w=w, kc=kc: e.matmul(ps[0][:], lhsT=w[:, kc * 128:(kc + 1) * 128], rhs=mix[:, kc, :],
                                                                  start=(kc == 0), stop=(kc == KC - 1)), reads=[wk, ("mix", kc), A], writes=[ps[1]])
            yo, yok = TG()
            p.add("act", lambda e, ps=ps, yo=yo: e.copy(out=yo[:], in_=ps[0][:]), reads=[ps[1]], writes=[yok])
            p.dma("sp", lambda e, yo=yo, m=m, half=half, part=part: e.dma_start(out=c.dram["ymix"][part, m, :, half * TH:(half + 1) * TH], in_=yo[:]),
                  sem="ymx%d" % yok[1], reads=[yok], writes=[("ymix", part, m, half)])

    for half in range(2):
        blocks = list(range(half * HB, (half + 1) * HB))
        for g in range(4):
            p.dma("sp", lambda e, g=g: e.dma_start(out=lng, in_=c.dram["lng"][:, g * GW:(g + 1) * GW]), sem="lng", reads=[A], writes=["lng"])
            p.dma("sp", lambda e, g=g: e.dma_start(out=lnb, in_=c.dram["lnb"][:, g * GW:(g + 1) * GW]), sem="lnb", reads=[A], writes=["lnb"])

            def cons_v(b, ps, g=g):
                v, vk = TG()
                p.add("act", lambda e, ps=ps, v=v: e.activation(out=v[:, 0:GW], in_=ps[0][:, 0:GW], func=AF.Gelu), reads=[ps[1]], writes=[vk])
                s1, s1k = sm[0], ("sm", 0)
                p.add("dve", lambda e, v=v, s1=s1: e.reduce_sum(out=s1[:, 0:1], in_=v[:, 0:GW], axis=mybir.AxisListType.X), reads=[vk, A], writes=[s1k])
                p.add("dve", lambda e, s1=s1: e.tensor_scalar(out=s1[:, 0:1], in0=s1[:, 0:1], scalar1=-1.0 / GW, scalar2=None, op0=ALU.mult),
                      reads=[s1k], writes=[s1k])
                p.add("dve", lambda e, v=v, s1=s1: e.tensor_scalar(out=v[:, 0:GW], in0=v[:, 0:GW], scalar1=s1[:, 0:1], scalar2=None, op0=ALU.add),
                      reads=[vk, s1k], writes=[vk])
                sq2, sq2k = TG()
                p.add("dve", lambda e, v=v, sq2=sq2: e.tensor_tensor(out=sq2[:, 0:GW], in0=v[:, 0:GW], in1=v[:, 0:GW], op=ALU.mult), reads=[vk], writes=[sq2k])
                p.add("dve", lambda e, sq2=sq2, s1=s1: e.reduce_sum(out=s1[:, 1:2], in_=sq2[:, 0:GW], axis=mybir.AxisListType.X), reads=[sq2k, s1k], writes=[s1k])
                p.add("act", lambda e, s1=s1: e.activation(out=s1[:, 1:2], in_=s1[:, 1:2], func=AF.Sqrt, scale=1.0 / GW, bias=c.eps_col[:, 0:1]),
                      reads=[s1k, "eps"], writes=[s1k])
                p.add("dve", lambda e, s1=s1: e.reciprocal(out=s1[:, 1:2], in_=s1[:, 1:2]), reads=[s1k], writes=[s1k])
                p.add("dve", lambda e, v=v, s1=s1: e.scalar_tensor_tensor(out=v[:, 0:GW], in0=v[:, 0:GW], scalar=s1[:, 1:2], in1=lng,
                                                                          op0=ALU.mult, op1=ALU.mult), reads=[vk, s1k, "lng"], writes=[vk])
                p.add("dve", lambda e, v=v, b=b: e.tensor_tensor(out=vnb[:, b % HB, :], in0=v[:, 0:GW], in1=lnb, op=ALU.add),
                      reads=[vk, "lnb", A], writes=[("vnb", b % HB)])
            tokmajor_proj("wv", g, blocks, cons_v)

            def cons_u(b, ps, g=g):
                gu, guk = TG()
                p.add("act", lambda e, ps=ps, gu=gu: e.activation(out=gu[:, 0:GW], in_=ps[0][:, 0:GW], func=AF.Gelu), reads=[ps[1]], writes=[guk])
                pss = c.bank()
                p.add("pe", lambda e, pss=pss, b=b, g=g: e.matmul(pss[0][:, 0:GW], lhsT=wsm[:, g, :], rhs=vnb[:, b % HB, :], start=True, stop=True),
                      reads=["wsm", ("vnb", b % HB), A], writes=[pss[1]])
                ao, aok = xds[ring["xd"] % 2], ("xd", ring["xd"] % 2)
                ring["xd"] += 1
                p.add("dve", lambda e, pss=pss, gu=gu, ao=ao, g=g: e.scalar_tensor_tensor(
                    out=ao[:, 0:GW], in0=pss[0][:, 0:GW], scalar=c.gbs[:, g:g + 1], in1=gu[:, 0:GW], op0=ALU.add, op1=ALU.mult),
                    reads=[pss[1], guk, "evc", A], writes=[aok])
                transpose_to_mix(ao, aok, mix, "mix", b, g)
            tokmajor_proj("wu", g, blocks, cons_u)
        out_proj(0, half)
        for g in range(SSD_G):
            for b in blocks:
                bs = slice(b * 128, (b + 1) * 128)
                pcb = c.bank()
                p.add("pe", lambda e, pcb=pcb, g=g, bs=bs: e.matmul(pcb[0][:, 0:128], lhsT=BT[:, g, bs], rhs=CT[:, g, bs], start=True, stop=True),
                      reads=[("BT", g, b // HB), ("CT", g, b // HB), A], writes=[pcb[1]])
                cbm, cbmk = cbms[ring["cbm"] % 2], ("cbm", ring["cbm"] % 2)
                ring["cbm"] += 1
                p.add("dve", lambda e, pcb=pcb, cbm=cbm: e.tensor_tensor(out=cbm[:, 0:128], in0=pcb[0][:, 0:128], in1=Umat, op=ALU.mult),
                      reads=[pcb[1], EC, A], writes=[cbmk])
                xdt, xdtk = xdts[ring["xdt"] % 2], ("xdt", ring["xdt"] % 2)
                ring["xdt"] += 1
                scale_heads(xdt, b, g, dtv, ("dtv", b), xdtk)
                ring["py"] = ring.get("py", 0) + 1
                py = c.pybanks[ring["py"] % 2]
                for r in range(HPG):
                    hh = g * HPG + r
                    pl = c.bank()
                    abc, abck = TG()
                    p.add("dve", lambda e, abc=abc, b=b, hh=hh: e.tensor_scalar(out=abc[:, 0:128], in0=onesf, scalar1=av[:, b, hh:hh + 1], scalar2=None,
                                                                                 op0=ALU.mult), reads=[("av", b), EC], writes=[abck])
                    p.add("pe", lambda e, pl=pl, abc=abc: e.matmul(pl[0][:, 0:128], lhsT=abc[:, 0:128], rhs=Umat, start=True, stop=False),
                          reads=[abck, EC], writes=[pl[1]])
                    p.add("pe", lambda e, pl=pl: e.matmul(pl[0][:, 0:128], lhsT=ident, rhs=NEGm, start=False, stop=True), reads=[EC], writes=[pl[1]])
                    lt, ltk = TG()
                    p.add("act", lambda e, pl=pl, lt=lt, b=b, hh=hh: e.activation(out=lt[:, 0:128], in_=pl[0][:, 0:128], func=AF.Exp,
                                                                                    bias=nacs[:, b, hh:hh + 1]),
                          reads=[pl[1], ("nacs", b)], writes=[ltk])
                    mt, mtk = TB()
                    p.add("dve", lambda e, lt=lt, cbm=cbm, mt=mt: e.tensor_tensor(out=mt[:, 0:128], in0=lt[:, 0:128], in1=cbm[:, 0:128], op=ALU.mult),
                          reads=[ltk, cbmk], writes=[mtk])
                    p.add("pe", lambda e, py=py, mt=mt, xdt=xdt, r=r: e.matmul(py[0][:, r * 64:(r + 1) * 64], lhsT=mt[:, 0:128], rhs=xdt[:, r * 64:(r + 1) * 64],
                                                                               start=True, stop=True), reads=[mtk, xdtk], writes=[py[1]])
                p.add("act", lambda e, g=g: e.copy(out=Sbf[:, 0:GW], in_=S[:, g, 0:GW]), reads=[("S", g), A], writes=["Sbf"])
                po = c.bank()
                p.add("pe", lambda e, po=po, g=g, bs=bs: e.matmul(po[0][:, 0:GW], lhsT=CT[:, g, bs], rhs=Sbf[:, 0:GW], start=True, stop=True),
                      reads=[("CT", g, b // HB), "Sbf", A], writes=[po[1]])
                y, yk = ybuf[b % HB], ("ybuf", b % HB)
                for r in range(HPG):
                    hh = g * HPG + r
                    p.add("dve", lambda e, y=y, po=po, r=r, hh=hh, b=b: e.tensor_scalar(
                        out=y[:, r * 64:(r + 1) * 64], in0=po[0][:, r * 64:(r + 1) * 64], scalar1=ea[:, b, hh:hh + 1], scalar2=None, op0=ALU.mult),
                        reads=[po[1], ("ea", b), A], writes=[yk])
                p.add("dve", lambda e, y=y, py=py: e.tensor_tensor(out=y[:, 0:GW], in0=py[0][:, 0:GW], in1=y[:, 0:GW], op=ALU.add), reads=[py[1], yk], writes=[yk])
                for r in range(HPG):
                    hh = g * HPG + r
                    p.add("dve", lambda e, y=y, r=r, hh=hh, b=b: e.scalar_tensor_tensor(
                        out=y[:, r * 64:(r + 1) * 64], in0=xs_tok[:, b, hh * 64:(hh + 1) * 64], scalar=c.hcon[:, 2 * H + hh:2 * H + hh + 1],
                        in1=y[:, r * 64:(r + 1) * 64], op0=ALU.mult, op1=ALU.add), reads=[("xs", b), "evc", yk, A], writes=[yk])
                state_step(b, g, block_states(b, g))

            def cons_z(b2, ps, g=g):
                y, yk = ybuf[b2 % HB], ("ybuf", b2 % HB)
                sz, szk = TG()
                p.add("act", lambda e, ps=ps, sz=sz: e.activation(out=sz[:, 0:GW], in_=ps[0][:, 0:GW], func=AF.Silu), reads=[ps[1]], writes=[szk])
                p.add("dve", lambda e, y=y, sz=sz: e.tensor_tensor(out=y[:, 0:GW], in0=y[:, 0:GW], in1=sz[:, 0:GW], op=ALU.mult), reads=[yk, szk], writes=[yk])
                p.add("dve", lambda e, y=y, sz=sz: e.tensor_tensor(out=sz[:, 0:GW], in0=y[:, 0:GW], in1=y[:, 0:GW], op=ALU.mult), reads=[yk, szk], writes=[szk])
                s1, s1k = sm[1], ("sm", 1)
                p.add("dve", lambda e, sz=sz, s1=s1: e.reduce_sum(out=s1[:, 0:1], in_=sz[:, 0:GW], axis=mybir.AxisListType.X), reads=[szk, A], writes=[s1k])
                p.add("act", lambda e, s1=s1: e.activation(out=s1[:, 0:1], in_=s1[:, 0:1], func=AF.Sqrt, scale=1.0 / GW, bias=c.eps_col[:, 0:1]),
                      reads=[s1k, "eps"], writes=[s1k])
                p.add("dve", lambda e, s1=s1: e.reciprocal(out=s1[:, 0:1], in_=s1[:, 0:1]), reads=[s1k], writes=[s1k])
                yb, ybk = xdts[ring["xdt"] % 2], ("xdt", ring["xdt"] % 2)
                ring["xdt"] += 1
                p.add("dve", lambda e, y=y, s1=s1, yb=yb: e.tensor_scalar(out=yb[:, 0:GW], in0=y[:, 0:GW], scalar1=s1[:, 0:1], scalar2=None, op0=ALU.mult),
                      reads=[yk, s1k, A], writes=[ybk])
                transpose_to_mix(yb, ybk, mix, "mix", b2, g, scale_col=c.snorm)
            tokmajor_proj("wz", g, blocks, cons_z)
        out_proj(1, half)
    fence(c)
    for kc in range(KC):
        p.dma("sp", lambda e, kc=kc: e.dma_start(out=c.xT[:, kc, :], in_=xsp[kc, :, :]), sem="xrl", batch=True,
              reads=[("xsp", kc), A], writes=[("x", kc, t) for t in range(NTH)])
    for kc in range(KC):
        for t in range(NTH):
            for part in range(2):
                yo, yok = TG()
                p.dma("sp", lambda e, yo=yo, kc=kc, t=t, part=part: e.dma_start(out=yo[:], in_=c.dram["ymix"][part, kc, :, t * TH:(t + 1) * TH]),
                      sem="yrl%d" % yok[1], reads=[("ymix", part, kc, t), A], writes=[yok])
                p.add("dve", lambda e, yo=yo, kc=kc, t=t: e.tensor_tensor(out=c.xT[:, kc, t * TH:(t + 1) * TH], in0=c.xT[:, kc, t * TH:(t + 1) * TH],
                                                                          in1=yo[:], op=ALU.add), reads=[yok, ("x", kc, t)], writes=[("x", kc, t)])


def emit_all(c, stages):
    p = c.p
    for kc in range(KC):
        for t in range(NTH):
            p.dma("sp", lambda e, kc=kc, t=t: e.dma_start(out=c.xT[:, kc, t * TH:(t + 1) * TH],
                                                          in_=c.dram["xT"][kc, :, t * TH:(t + 1) * TH]),
                  sem="xin", writes=[("x", kc, t)], batch=True)
    p.dma("sp", lambda e: e.dma_start(out=c.gain_all[:], in_=c.dram["gains"][:, :]), sem="const", writes=["gains"], batch=True)
    for nm in ("swac", "sinkb", "rotT", "bk2", "bq", "bv2", "masks", "sel"):
        p.dma("sp", lambda e, nm=nm: e.dma_start(out=getattr(c, nm + "_ld")[:], in_=c.dram[nm][:, :]), sem="const",
              writes=[("cld", nm)], batch=True)
    CL = [("cld", nm) for nm in ("swac", "sinkb", "rotT", "bk2", "bq", "bv2", "masks", "sel")]
    p.add("dve", lambda e: e.memset(c.dummy[:], 0.0), reads=["swac"] + CL, writes=["swac", "sel", "dummy"])
    ECN = ("hcon", "cw", "cb", "wsT", "gbs", "snorm", "econ", "ltm", "identb")
    for nm in ECN:
        p.dma("sp", lambda e, nm=nm: e.dma_start(out=getattr(c, nm)[:], in_=c.dram[nm][:, :]), sem="const", writes=[("cld", nm)], batch=True)
    p.add("dve", lambda e: e.memset(c.one_col[:], 1.0), reads=[("cld", nm) for nm in ECN], writes=["evc", "econ", "eps2"])
    p.add("dve", lambda e: e.memset(c.ones_bf[:], 1.0), writes=["ones"])
    p.add("dve", lambda e: e.memset(c.eps_col[:], EPS), writes=["eps"])
    for st in stages:
        if st[0] == "ffn":
            ffn(c, st[1], st[2])
        elif st[0] == "xattn":
            xattn(c, st[1])
        elif st[0] == "swa":
            swa(c, 0)
        elif st[0] == "even":
            even_mixer(c)
        elif st[0] == "final":
            rmsnorm(c, c.xT, xkey, T, c.gains[("final", 0)], c.xT, xkey)
    toks = []
    for kc in range(KC):
        rd = [("x", kc, t) for t in range(NTH)]
        toks.append(p.dma("sp", lambda e, kc=kc: e.dma_start(out=c.dram["outT"][kc, :, :], in_=c.xT[:, kc, :]),
                          sem="out", reads=rd, batch=True))
    return [toks[-1]]


GAIN_NAMES = [("ffn1", 0), ("ffn1", 1), ("mix", 0), ("mix", 1), ("xq", 0), ("xq", 1), ("mem", 0), ("mem", 1),
              ("ffn2", 0), ("ffn2", 1), ("final", 0)]
ARENA_F32 = 15400


def build_program(stages):
    nc = bass.Bass("TRN2", target_bir_lowering=False)
    c = Ctx()
    c.nc = nc
    dram = {}

    def din(name, shape, dt=F32):
        dram[name] = nc.dram_tensor(name, shape, dt, kind="ExternalInput").ap()
    din("xT", [KC, 128, T])
    din("memT", [KC, 128, NMEM])
    din("gains", [128, len(GAIN_NAMES) * KC])
    for w in (1, 2):
        din("wgu%d" % w, [2, 2 * NFF, 128, KC * 128])
        din("wdn%d" % w, [2, FF_GROUPS, KC, 128, FPG * 128])
    din("wxq", [2, XH, 128, KC * 128])
    din("wxk", [2, XH, 128, KC * 128])
    din("wxv", [2, KC // 4, 128, 4 * 512])
    din("wxo", [2, KC // 4, 128, 4 * XH * 128])
    din("swac", [128, 4])
    din("posi", [128, T], I32)
    din("sinkb", [128, D // HD])
    din("rotT", [128, 128], BF16)
    din("bk2", [128, NKV])
    din("bq", [128, KC])
    din("bv2", [128, 512])
    din("masks", [128, 768], BF16)
    din("sel", [128, NCORES])
    din("wqt", [1, KC, 128, KC * 128])
    din("wk2", [1, NKV, 128, KC * 128])
    din("wv2", [1, KC // 4, 128, 4 * 512])
    din("wo", [1, KC, 128, KC * 128])
    H_ = D // 64
    CD_ = D + 1024
    NT_ = CD_ // 128
    GW_ = D // 4
    KCS_ = min(KC, WSLOT // GW_)
    NTG_ = KC // KCS_
    din("wxbc", [NT_, 128, KC * 128])
    din("wdt", [128, KC * H_])
    for nm in ("wu", "wv", "wz"):
        din(nm, [4, NTG_, 128, KCS_ * GW_])
    din("wout", [2, KC, 128, KC * 128])
    din("lng", [128, D])
    din("lnb", [128, D])
    din("hcon", [128, 3 * H_])
    din("cw", [128, NT_ * 4])
    din("cb", [128, NT_])
    din("wsT", [128, 512])
    din("gbs", [128, 4])
    din("snorm", [128, KC])
    din("econ", [128, 512])
    din("ltm", [128, NCORES])
    din("identb", [128, 128], BF16)
    dram["xsp"] = nc.dram_tensor("xsp", [KC, 128, T], F32, kind="Internal").ap()
    dram["ymix"] = nc.dram_tensor("ymix", [2, KC, 128, T], F32, kind="Internal").ap()
    for key, ncol in (("swa", 1024), ("cv", 128), ("st", 2048 + 64)):
        dram["cin_" + key] = nc.dram_tensor("cin_" + key, [128, ncol], F32, kind="Internal").ap()
        dram["cout_" + key] = nc.dram_tensor("cout_" + key, [NCORES * 128, ncol], F32, kind="Internal").ap()
    dram["outT"] = nc.dram_tensor("outT", [KC, 128, T], F32, kind="ExternalOutput").ap()
    c.dram = dram
    stack = contextlib.ExitStack()
    with stack:
        sb = lambda name, shape, dt: stack.enter_context(nc.sbuf_tensor(name, shape, dt))
        c.xT = sb("xT_sb", [128, KC, T], F32)
        c.hT = sb("hT_sb", [128, KC, T], BF16)
        arena = sb("arena", [128, ARENA_F32], F32)
        c.arena_f = lambda off, n: arena[:, off:off + n]
        c.arena_bf = lambda off, n: arena[:, off:off + (n + 1) // 2].bitcast(BF16)[:, 0:n]
        c.sq = [sb("sq%d" % i, [128, TH], BF16) for i in range(2)]
        tmps = [sb("tmp%d" % i, [128, TH], F32) for i in range(3)]
        c.rstd = sb("rstd", [128, TH], F32)
        c.ones_bf = sb("ones_bf", [128, 128], BF16)
        c.eps_col = sb("eps_col", [128, 1], F32)
        c.dummy = sb("dummy_f", [128, 1], F32)
        c.swac_ld = sb("swac_sb", [128, 4], F32); c.swac = c.swac_ld
        c.sinkb_ld = sb("sinkb_sb", [128, D // HD], F32); c.sinkb = c.sinkb_ld
        c.rotT_ld = sb("rotT_sb", [128, 128], BF16); c.rotT = c.rotT_ld
        c.bk2_ld = sb("bk2_sb", [128, NKV], F32); c.bk2 = c.bk2_ld
        c.bq_ld = sb("bq_sb", [128, KC], F32); c.bq = c.bq_ld
        c.bv2_ld = sb("bv2_sb", [128, 512], F32); c.bv2 = c.bv2_ld
        c.masks_ld = sb("masks_sb", [128, 768], BF16); c.masks = c.masks_ld
        c.sel_ld = sb("sel_sb", [128, NCORES], F32); c.sel = c.sel_ld
        c.pts = [sb("pts%d" % i, [128, 256], BF16) for i in range(4)]
        for nm, shp in (("hcon", [128, 3 * H_]), ("cw", [128, NT_ * 4]), ("cb", [128, NT_]), ("wsT", [128, 512]),
                        ("gbs", [128, 4]), ("snorm", [128, KC]), ("econ", [128, 512]), ("ltm", [128, NCORES])):
            setattr(c, nm, sb(nm + "_sb", shp, F32))
        c.one_col = sb("one_col", [128, 1], F32)
        c.identb = sb("identb_sb", [128, 128], BF16)
        c.gain_all = sb("gain_all", [128, len(GAIN_NAMES) * KC], F32)
        c.gains = {nm: i * KC for i, nm in enumerate(GAIN_NAMES)}
        slots = [sb("wslot%d" % i, [128, WSLOT], BF16) for i in range(NWSLOT)]
        banks = [stack.enter_context(nc.psum_tensor("bank%d" % i, [128, TH], F32)) for i in range(8)]

        def bank():
            b = banks[c.bank_i % 6]
            k = ("bank", c.bank_i % 6)
            c.bank_i += 1
            return (b, k)
        c.bank = bank
        c.pybanks = [(banks[6], ("bank", 6)), (banks[7], ("bank", 7))]

        def tmpf():
            t_ = tmps[c.tmp_i % 3]
            k = ("tmp", c.tmp_i % 3)
            c.tmp_i += 1
            return t_, k
        c.tmpf = tmpf
        plan = None
        for dry in (True, False):
            c.p = Prog(nc, dry=dry)
            c.ws = WStream(c.p, slots, plan)
            c.bank_i = 0
            c.tmp_i = 0
            fin = emit_all(c, stages)
            plan = c.ws.req
        c.p.finalize(stack, fin)
    return nc


def host_prep(inputs):
    f = lambda a: np.ascontiguousarray(np.asarray(a, dtype=np.float32))
    g = lambda k: np.asarray(inputs[k], dtype=np.float32)
    shared = {}
    gl = []
    for nm, li in GAIN_NAMES:
        key = {"ffn1": "norm_ffn1", "mix": "norm_mix", "xq": "norm_xq", "mem": "norm_mem", "ffn2": "norm_ffn2",
               "final": "final_norm"}[nm]
        gg = g(key)
        gg = gg if nm == "final" else gg[li]
        gl.append(gg.reshape(KC, 128).T)
    shared["gains"] = f(np.concatenate(gl, axis=1))
    ffn_w = {1: (inputs["w_ffn1_gu"], inputs["w_ffn1_down"]), 2: (inputs["w_ffn2_gu"], inputs["w_ffn2_down"])}
    for w in (1, 2):
        wgu = np.asarray(ffn_w[w][0], dtype=np.float32)
        shared["wgu%d" % w] = f(wgu.reshape(2, KC, 128, 2 * NFF, 128).transpose(0, 3, 2, 1, 4).reshape(2, 2 * NFF, 128, KC * 128))
        wdn = np.asarray(ffn_w[w][1], dtype=np.float32)
        shared["wdn%d" % w] = f(wdn.reshape(2, FF_GROUPS, FPG, 128, KC, 128).transpose(0, 1, 4, 3, 2, 5).reshape(2, FF_GROUPS, KC, 128, FPG * 128))
    wxq = g("w_xq")
    shared["wxq"] = f(wxq.reshape(2, KC, 128, XH, 128).transpose(0, 3, 2, 1, 4).reshape(2, XH, 128, KC * 128))
    wxkv = g("w_xkv")
    shared["wxk"] = f(wxkv[:, :, 0:512].reshape(2, KC, 128, XH, 128).transpose(0, 3, 2, 1, 4).reshape(2, XH, 128, KC * 128))
    shared["wxv"] = f(wxkv[:, :, 512:1024].reshape(2, KC // 4, 4, 128, 512).transpose(0, 1, 3, 2, 4).reshape(2, KC // 4, 128, 4 * 512))
    wxo = g("w_xo")
    shared["wxo"] = f(wxo.reshape(2, XH, 128, KC // 4, 4, 128).transpose(0, 3, 2, 4, 1, 5).reshape(2, KC // 4, 128, 4 * XH * 128))
    shared["memT"] = f(g("mem")[0].T.reshape(KC, 128, NMEM))
    NH = D // HD
    wqkv = g("w_qkv")[0]
    bqkv = g("b_qkv")[0]
    til = lambda wm: wm.reshape(KC, 128, -1, 128).transpose(2, 1, 0, 3).reshape(-1, 128, KC * 128)
    shared["wqt"] = f(til(wqkv[:, 0:D])[None])
    wk = wqkv[:, D:D + NKV * HD].reshape(D, NKV, 1, HD)
    shared["wk2"] = f(til(np.broadcast_to(wk, (D, NKV, 2, HD)).reshape(D, NKV * 128))[None])
    wv = wqkv[:, D + NKV * HD:].reshape(D, NKV, 1, HD)
    wv2 = np.broadcast_to(wv, (D, NKV, 2, HD)).reshape(D, 512)
    shared["wv2"] = f(wv2.reshape(KC // 4, 4, 128, 512).transpose(0, 2, 1, 3).reshape(1, KC // 4, 128, 4 * 512))
    shared["wo"] = f(til(g("w_o_odd")[0])[None])
    shared["bq"] = f(bqkv[0:D].reshape(KC, 128).T)
    bk = bqkv[D:D + NKV * HD].reshape(NKV, 1, HD)
    shared["bk2"] = f(np.broadcast_to(bk, (NKV, 2, HD)).reshape(NKV, 128).T)
    bv = bqkv[D + NKV * HD:].reshape(NKV, 1, HD)
    shared["bv2"] = f(np.broadcast_to(np.broadcast_to(bv, (NKV, 2, HD)).reshape(1, 512), (128, 512)))
    shared["sinkb"] = f(np.broadcast_to(g("sinks")[0][None, :], (128, NH)))
    dd = np.arange(128) % HD
    inv_freq = 500000.0 ** (-np.arange(0, 16, 2, dtype=np.float32) / 16.0)
    swac = np.zeros((128, 4), np.float32)
    swac[:, 0] = np.where(dd < 16, inv_freq[dd % 8], 0.0) / (2.0 * np.pi)
    shared["swac"] = swac
    rot = np.zeros((128, 128), np.float32)
    for m_ in range(128):
        if m_ % HD < 8:
            rot[m_ + 8, m_] = -1.0
        elif m_ % HD < 16:
            rot[m_ - 8, m_] = 1.0
    shared["rotT"] = rot.astype(BF16_NP)
    jj = np.arange(128)[:, None]
    ii = np.arange(128)[None, :]
    mprev = np.tile((jj > ii).astype(np.float32), (1, 2))
    mcur = np.tile((jj <= ii).astype(np.float32), (1, 2))
    H_ = D // 64
    CD_ = D + 1024
    NT_ = CD_ // 128
    GW_ = D // 4
    KCS_ = min(KC, WSLOT // GW_)
    NTG_ = KC // KCS_
    win = g("w_in_even")[0]
    shared["wxbc"] = f(til(win[:, 3 * D:3 * D + CD_]))
    shared["wdt"] = f(win[:, 3 * D + CD_:].reshape(KC, 128, H_).transpose(1, 0, 2).reshape(128, KC * H_))
    tm = lambda wm: wm.reshape(NTG_, KCS_, 128, 4, GW_).transpose(3, 0, 2, 1, 4).reshape(4, NTG_, 128, KCS_ * GW_)
    shared["wu"] = f(tm(win[:, 0:D]))
    shared["wv"] = f(tm(win[:, D:2 * D]))
    shared["wz"] = f(tm(win[:, 2 * D:3 * D]))
    wout = g("w_out_even")[0]
    shared["wout"] = f(np.stack([til(wout[0:D]), til(wout[D:2 * D])]))
    rep = lambda v: np.broadcast_to(np.asarray(v, np.float32)[None, :], (128, len(v)))
    shared["lng"] = f(rep(g("gm_ln_g")[0]))
    shared["lnb"] = f(rep(g("gm_ln_b")[0]))
    shared["hcon"] = f(np.concatenate([rep(g("dt_bias")[0]), rep(g("a_log")[0]), rep(g("d_skip")[0])], axis=1))
    shared["cw"] = f(g("conv_w")[0].reshape(4, NT_, 128).transpose(2, 1, 0).reshape(128, NT_ * 4))
    shared["cb"] = f(g("conv_b")[0].reshape(NT_, 128).T)
    shared["wsT"] = f(g("gm_ws")[0].transpose(2, 0, 1).reshape(128, 512))
    shared["gbs"] = f(g("gm_bs")[0].T)
    shared["snorm"] = f(g("ssd_norm")[0].reshape(KC, 128).T)
    jj = np.arange(128)[:, None]
    ii = np.arange(128)[None, :]
    causal = (jj <= ii).astype(np.float32)
    shared["econ"] = f(np.concatenate([causal, (jj > ii).astype(np.float32) * -30000.0, np.eye(128, dtype=np.float32),
                                       np.ones((128, 128), np.float32)], axis=1))
    x = g("x")[0]
    in_maps = []
    for cix in range(NCORES):
        m = dict(shared)
        m["xT"] = f(x[cix * T:(cix + 1) * T].T.reshape(KC, 128, T))
        pos = np.asarray(inputs["positions"]).astype(np.int32)[0, cix * T:(cix + 1) * T]
        m["posi"] = np.ascontiguousarray(np.broadcast_to(pos[None, :], (128, T)))
        m["masks"] = np.ascontiguousarray(np.concatenate([mprev, mcur, mprev if cix > 0 else np.zeros_like(mprev)], axis=1)).astype(BF16_NP)
        sel = np.zeros((128, NCORES), np.float32)
        if cix > 0:
            sel[:, cix - 1] = 1.0
        m["sel"] = sel
        m["identb"] = np.eye(128, dtype=np.float32).astype(BF16_NP)
        m["ltm"] = np.ascontiguousarray(np.broadcast_to((np.arange(NCORES) < cix).astype(np.float32)[None, :], (128, NCORES)))
        in_maps.append(m)
    return in_maps


def run_stages(inputs, stages, trace=False):
    nc = build_program(stages)
    in_maps = host_prep(inputs)
    res = run_bass_kernel_spmd(nc, in_maps, core_ids=list(range(NCORES)), trace=trace)
    outs = [r["outT"].reshape(D, T).T for r in res.results]
    return np.ascontiguousarray(np.concatenate(outs, axis=0)[None]), res


FULL_STAGES = [("ffn", 0, 1), ("even",), ("xattn", 0), ("ffn", 0, 2), ("ffn", 1, 1), ("swa",), ("xattn", 1), ("ffn", 1, 2), ("final",)]


def kernel(**inputs):
    stages = FULL_STAGES
    out, _ = run_stages(inputs, stages)
    return out.astype(np.float32)
```
